# Optimizing a Trainium2 kernel written in Bass

```python
import math
import jax
import jax.numpy as jnp
from jax import lax
import numpy as np

D_MODEL = 1024
BATCH = 4
SEQ = 4096
DEPTH = 2

N_EVEN = (DEPTH + 1) // 2
N_ODD = DEPTH // 2
MIX_WIDTH = D_MODEL
D_FF = 4 * D_MODEL
DEEPNORM_ALPHA = (2.0 * DEPTH) ** 0.25
DEEPNORM_BETA = (8.0 * DEPTH) ** -0.25
LN_EPS = 1e-5
RMS_EPS = 1e-6

HY_DIM = MIX_WIDTH // 2
HY_ORDER = 2
HY_SHORT = 3
HY_EMB = 33
HY_BANDS = (HY_EMB - 1) // 2
HY_FILT_HID = 64
HY_DECAY_TARGET = 1e-2
HY_FAST_DECAY_PCT = 0.3
HY_SLOW_DECAY_PCT = 1.5

GDN_HEADS = 4
GDN_DK = 128
GDN_DV = (MIX_WIDTH - HY_DIM) // GDN_HEADS
GDN_CONV = 5
GDN_CHUNK = 64

HG_HEADS = 4
HG_DK = 128
HG_DV = 128
HG_CHUNK = 64

RW_HEADS = 8
RW_HD = 64
RW_DIM = RW_HEADS * RW_HD
RW_W_LORA = 64
RW_A_LORA = 64
RW_G_LORA = 128
RW_GN_EPS = 64e-5

EVEN_SPLITS = (3 * HY_DIM, GDN_HEADS * GDN_DK, GDN_HEADS * GDN_DK, GDN_HEADS * GDN_DV,
               GDN_HEADS * GDN_DV, 2 * GDN_HEADS, GDN_HEADS)
EVEN_IN = sum(EVEN_SPLITS)
RW_SPLITS = (RW_DIM, RW_DIM, RW_DIM, RW_W_LORA, RW_A_LORA, RW_G_LORA)
RW_IN = sum(RW_SPLITS)
ODD_SPLITS = (HG_HEADS * HG_DK, 2 * HG_HEADS * HG_DK, HG_HEADS * HG_DV, HG_HEADS * HG_DV, RW_IN)
ODD_IN = sum(ODD_SPLITS)

kernel_name = 'hybrid_bidir_hyena_gdn_hgrn2_rwkv7'


def split_cols(y, sizes):
    return jnp.split(y, [int(s) for s in np.cumsum(sizes)[:-1]], axis=-1)


def layer_norm(x, g, b):
    xf = x.astype(jnp.float32)
    xc = xf - jnp.mean(xf, -1, keepdims=True)
    var = jnp.mean(xc * xc, -1, keepdims=True)
    return (xc * lax.rsqrt(var + LN_EPS) * g + b).astype(x.dtype)


def rms_norm(x, g):
    return x * lax.rsqrt(jnp.mean(x * x, -1, keepdims=True) + RMS_EPS) * g


def l2_normalize(x):
    return x * lax.rsqrt(jnp.sum(x * x, -1, keepdims=True) + 1e-6)


def centred_dwconv(u, w):
    width = w.shape[0]
    return lax.conv_general_dilated(u, w[:, None, :].astype(u.dtype), (1,), [(width // 2, width // 2)],
                                    dimension_numbers=('NWC', 'WIO', 'NWC'),
                                    feature_group_count=u.shape[-1])


def to_chunks(a, chunk):
    z, b, t, h = a.shape[:4]
    a = a.reshape((z, b, t // chunk, chunk, h) + a.shape[4:])
    return jnp.swapaxes(a, 3, 4)


def from_chunks(a):
    a = jnp.swapaxes(a, 3, 4)
    z, b, n, c, h = a.shape[:5]
    return a.reshape((z, b, n * c, h) + a.shape[5:])


def both_directions(a):
    return jnp.stack([a, jnp.flip(a, 1)], 0)


def merge_directions(o):
    return o[0] + jnp.flip(o[1], 1)


def hyena_filters(seq_len, w1, b1, w2, b2, w3, b3, w4, freq):
    pos = jnp.arange(seq_len, dtype=jnp.float32)[:, None]
    t = pos / max(seq_len - 1, 1)
    bands = jnp.linspace(1e-4, HY_BANDS - 1, HY_BANDS, dtype=jnp.float32)[None, :]
    ang = bands * (2.0 * math.pi / seq_len) * pos
    z = jnp.concatenate([t, jnp.cos(ang), -jnp.sin(ang)], -1)
    h = jnp.sin(freq * (z @ w1 + b1))
    h = jnp.sin(freq * (h @ w2 + b2))
    h = jnp.sin(freq * (h @ w3 + b3))
    h = (h @ w4).reshape(seq_len, HY_ORDER, 2, HY_DIM)
    max_decay = math.log(HY_DECAY_TARGET) / HY_FAST_DECAY_PCT
    min_decay = math.log(HY_DECAY_TARGET) / HY_SLOW_DECAY_PCT
    deltas = jnp.abs(jnp.linspace(min_decay, max_decay, HY_DIM, dtype=jnp.float32))
    window = jnp.exp(-t * deltas)
    return h * window[:, None, None, :]


def hyena_long_conv(u, h_fwd, h_bwd, skip):
    seq_len = u.shape[1]
    h_two = jnp.concatenate([h_fwd.at[0].add(h_bwd[0]), jnp.zeros_like(h_fwd[:1]), h_bwd[:0:-1]], axis=0)
    u_f = jnp.fft.rfft(u, n=2 * seq_len, axis=1)
    h_f = jnp.fft.rfft(h_two, n=2 * seq_len, axis=0)
    y = jnp.fft.irfft(u_f * h_f[None], n=2 * seq_len, axis=1)[:, :seq_len]
    return y + u * skip


def hyena_mixer(u, conv_w, conv_b, filt, skip):
    u = centred_dwconv(u, conv_w) + conv_b
    x1, x2, v = jnp.split(u, 3, axis=-1)
    z = v
    for o, gate in enumerate((x1, x2)):
        z = gate * hyena_long_conv(z, filt[:, o, 0], filt[:, o, 1], skip[o])
    return z


def gated_delta_chunked(q, k, v, beta, log_g):
    c = GDN_CHUNK
    q, k, v = (to_chunks(a, c) for a in (q, k, v))
    beta, log_g = (to_chunks(a, c) for a in (beta, log_g))
    dv = v.shape[-1]
    gam = jnp.cumsum(log_g, axis=-1)
    causal = jnp.tril(jnp.ones((c, c), dtype=bool))
    strict = jnp.tril(jnp.ones((c, c), dtype=bool), -1)
    decay = jnp.exp(jnp.where(causal, gam[..., :, None] - gam[..., None, :], -jnp.inf))
    kk = jnp.einsum('zbnhtd,zbnhsd->zbnhts', k, k)
    a_mat = jnp.where(strict, kk * decay * beta[..., :, None], 0.0)
    rhs = jnp.concatenate([v * beta[..., None], k * (beta * jnp.exp(gam))[..., None]], -1)
    sol = lax.linalg.triangular_solve(a_mat + jnp.eye(c, dtype=a_mat.dtype), rhs,
                                      left_side=True, lower=True, unit_diagonal=True)
    u, w = sol[..., :dv], sol[..., dv:]
    attn = jnp.einsum('zbnhtd,zbnhsd->zbnhts', q, k) * decay
    q_dec = q * jnp.exp(gam)[..., None]
    g_last = gam[..., -1]
    k_dec = k * jnp.exp(g_last[..., None] - gam)[..., None]

    def step(state, inp):
        q_c, a_c, u_c, w_c, k_c, g_c = inp
        v_new = u_c - jnp.einsum('zbhcd,zbhde->zbhce', w_c, state)
        out = jnp.einsum('zbhcd,zbhde->zbhce', q_c, state) + jnp.einsum('zbhts,zbhse->zbhte', a_c, v_new)
        state = state * jnp.exp(g_c)[..., None, None] + jnp.einsum('zbhcd,zbhce->zbhde', k_c, v_new)
        return state, out

    xs = tuple(jnp.moveaxis(a, 2, 0) for a in (q_dec, attn, u, w, k_dec, g_last))
    z, b, _, h, _, dk = q.shape
    state0 = jnp.zeros((z, b, h, dk, dv), q.dtype)
    _, out = lax.scan(step, state0, xs)
    return from_chunks(jnp.moveaxis(out, 0, 2))


def gdn_mixer(q, k, v, zgate, a, b, conv_w, a_log, dt_bias, norm_w):
    bsz, seq = q.shape[:2]
    qkv = jax.nn.silu(centred_dwconv(jnp.concatenate([q, k, v], -1), conv_w))
    q, k, v = split_cols(qkv, (GDN_HEADS * GDN_DK, GDN_HEADS * GDN_DK, GDN_HEADS * GDN_DV))
    q = l2_normalize(q.reshape(bsz, seq, GDN_HEADS, GDN_DK)) * GDN_DK ** -0.5
    k = l2_normalize(k.reshape(bsz, seq, GDN_HEADS, GDN_DK))
    v = v.reshape(bsz, seq, GDN_HEADS, GDN_DV)
    beta = jax.nn.sigmoid(b)
    log_g = -jnp.exp(a_log) * jax.nn.softplus(a.reshape(bsz, seq, 2, GDN_HEADS) + dt_bias)
    log_g = jnp.stack([log_g[:, :, 0], jnp.flip(log_g[:, :, 1], 1)], 0)
    o = gated_delta_chunked(both_directions(q), both_directions(k), both_directions(v),
                            both_directions(beta), log_g)
    o = merge_directions(o)
    o = rms_norm(o, norm_w) * jax.nn.silu(zgate.reshape(bsz, seq, GDN_HEADS, GDN_DV))
    return o.reshape(bsz, seq, GDN_HEADS * GDN_DV)


def hgrn2_chunked(q, k, v, log_f):
    c = HG_CHUNK
    q, k, v, log_f = (to_chunks(a, c) for a in (q, k, v, log_f))
    gam = jnp.cumsum(log_f, axis=-2)
    causal = jnp.tril(jnp.ones((c, c), dtype=bool))[..., None]

    def step(state, inp):
        q_c, k_c, v_c, g_c = inp
        inter = jnp.einsum('zbhcd,zbhde->zbhce', q_c * jnp.exp(g_c), state)
        dec = jnp.exp(jnp.where(causal, g_c[..., :, None, :] - g_c[..., None, :, :], -jnp.inf))
        attn = jnp.einsum('zbhtd,zbhsd,zbhtsd->zbhts', q_c, k_c, dec)
        out = inter + jnp.einsum('zbhts,zbhse->zbhte', attn, v_c)
        g_last = g_c[..., -1:, :]
        state = (state * jnp.exp(g_last)[..., 0, :, None]
                 + jnp.einsum('zbhcd,zbhce->zbhde', k_c * jnp.exp(g_last - g_c), v_c))
        return state, out

    xs = tuple(jnp.moveaxis(a, 2, 0) for a in (q, k, v, gam))
    z, b, _, h, _, dk = q.shape
    state0 = jnp.zeros((z, b, h, dk, v.shape[-1]), q.dtype)
    _, out = lax.scan(step, state0, xs)
    return from_chunks(jnp.moveaxis(out, 0, 2))


def hgrn2_mixer(q, f, i, g, lower_bound, norm_w):
    bsz, seq = q.shape[:2]

    def heads(a, d):
        return a.reshape(bsz, seq, HG_HEADS, d)

    q = heads(jax.nn.silu(q), HG_DK)
    fg = lower_bound + (1.0 - lower_bound) * jax.nn.sigmoid(f.reshape(bsz, seq, 2, HG_HEADS * HG_DK))
    fg = fg.reshape(bsz, seq, 2, HG_HEADS, HG_DK)
    fz = jnp.stack([fg[:, :, 0], jnp.flip(fg[:, :, 1], 1)], 0)
    o = hgrn2_chunked(both_directions(q), 1.0 - fz, both_directions(heads(i, HG_DV)), jnp.log(fz))
    o = merge_directions(o)
    o = rms_norm(o, norm_w) * jax.nn.silu(heads(g, HG_DV))
    return o.reshape(bsz, seq, HG_HEADS * HG_DV)


def rwkv7_mixer(u, mu, w0, w2, a0, a2, g2, k_k, k_a, r_k, lnx_w, lnx_b):
    bsz, seq = u.shape[:2]
    prev = jnp.pad(u, ((0, 0), (1, 0), (0, 0)))[:, :-1]
    nxt = jnp.pad(u, ((0, 0), (0, 1), (0, 0)))[:, 1:]
    u = u + mu[0] * (prev - u) + mu[1] * (nxt - u)
    r, k, v, w_lo, a_lo, g_lo = split_cols(u, RW_SPLITS)
    w_raw = w0[:, None, None, :] + jnp.einsum('btr,zrc->zbtc', jnp.tanh(w_lo), w2)
    log_w = -jnp.exp(-jax.nn.softplus(-w_raw) - 0.5)
    a = jax.nn.sigmoid(a0 + a_lo @ a2)
    g = jax.nn.sigmoid(g_lo) @ g2

    def heads(t):
        return t.reshape(t.shape[:-1] + (RW_HEADS, RW_HD))

    kk = l2_normalize(heads(k * k_k))
    k = k * (1.0 + (a - 1.0) * k_a)
    r, k, v, a = heads(r), heads(k), heads(v), heads(a)
    decay = jnp.exp(heads(log_w))
    decay = jnp.stack([decay[0], jnp.flip(decay[1], 1)], 0)
    xs = (both_directions(r), decay, both_directions(k), both_directions(v),
          both_directions(kk), both_directions(kk * a))
    xs = tuple(jnp.moveaxis(t, 2, 0) for t in xs)

    def step(state, inp):
        r_t, w_t, k_t, v_t, kk_t, kka_t = inp
        sa = jnp.einsum('zbhvk,zbhk->zbhv', state, kk_t)
        state = (state * w_t[..., None, :] - sa[..., :, None] * kka_t[..., None, :]
                 + v_t[..., :, None] * k_t[..., None, :])
        return state, jnp.einsum('zbhvk,zbhk->zbhv', state, r_t)

    state0 = jnp.zeros((2, bsz, RW_HEADS, RW_HD, RW_HD), r.dtype)
    _, y = lax.scan(step, state0, xs)
    y = merge_directions(jnp.moveaxis(y, 0, 2))
    yc = y - jnp.mean(y, -1, keepdims=True)
    y = yc * lax.rsqrt(jnp.mean(yc * yc, -1, keepdims=True) + RW_GN_EPS)
    y = y.reshape(bsz, seq, RW_DIM) * lnx_w + lnx_b
    bonus = jnp.sum(r * k * r_k, -1, keepdims=True) * v
    return (y + bonus.reshape(bsz, seq, RW_DIM)) * g


def even_layer_mixer(x, w_in, hy_conv_w, hy_conv_b, f_w1, f_b1, f_w2, f_b2, f_w3, f_b3, f_w4, f_freq,
                     hy_skip, gdn_conv_w, gdn_a_log, gdn_dt_bias, gdn_norm_w, w_out):
    proj = (x @ w_in).astype(jnp.float32)
    hy_in, q, k, v, zg, a, b = split_cols(proj, EVEN_SPLITS)
    filt = hyena_filters(x.shape[1], f_w1, f_b1, f_w2, f_b2, f_w3, f_b3, f_w4, f_freq)
    y_a = hyena_mixer(hy_in, hy_conv_w, hy_conv_b, filt, hy_skip)
    y_b = gdn_mixer(q, k, v, zg, a, b, gdn_conv_w, gdn_a_log, gdn_dt_bias, gdn_norm_w)
    return jnp.concatenate([y_a, y_b], -1).astype(x.dtype) @ w_out


def odd_layer_mixer(x, w_in, lower_bound, hg_norm_w, rw_mu, rw_w0, rw_w2, rw_a0, rw_a2, rw_g2,
                    rw_k_k, rw_k_a, rw_r_k, rw_lnx_w, rw_lnx_b, w_out):
    proj = (x @ w_in).astype(jnp.float32)
    q, f, i, g, rw_in = split_cols(proj, ODD_SPLITS)
    y_c = hgrn2_mixer(q, f, i, g, lower_bound, hg_norm_w)
    y_d = rwkv7_mixer(rw_in, rw_mu, rw_w0, rw_w2, rw_a0, rw_a2, rw_g2, rw_k_k, rw_k_a, rw_r_k,
                      rw_lnx_w, rw_lnx_b)
    return jnp.concatenate([y_c, y_d], -1).astype(x.dtype) @ w_out


def sq_relu_mlp(x, w1, w2):
    return jnp.square(jax.nn.relu(x @ w1)) @ w2


def setup_inputs(seed: int = 0) -> dict:
    key = jax.random.key(seed)
    keys = list(jax.random.split(key, 48))

    def nrm(shape, scale):
        return jax.random.normal(keys.pop(), shape, jnp.float32) * scale

    def uni(shape, lo, hi):
        return jax.random.uniform(keys.pop(), shape, jnp.float32, lo, hi)

    beta = DEEPNORM_BETA
    ev_scale = jnp.concatenate([
        jnp.ones((2 * HY_DIM,)), jnp.full((HY_DIM,), beta),
        jnp.ones((2 * GDN_HEADS * GDN_DK,)), jnp.full((GDN_HEADS * GDN_DV,), beta),
        jnp.ones((GDN_HEADS * GDN_DV + 3 * GDN_HEADS,))])
    od_scale = jnp.concatenate([
        jnp.ones((3 * HG_HEADS * HG_DK,)), jnp.full((HG_HEADS * HG_DV,), beta),
        jnp.ones((HG_HEADS * HG_DV + 2 * RW_DIM,)), jnp.full((RW_DIM,), beta),
        jnp.ones((RW_W_LORA + RW_A_LORA + RW_G_LORA,))])
    dt = jnp.exp(uni((N_EVEN, 2, GDN_HEADS), math.log(1e-3), math.log(1e-1)))
    return {
        'x': nrm((BATCH, SEQ, D_MODEL), 1.0),
        'ev_w_in': nrm((N_EVEN, D_MODEL, EVEN_IN), D_MODEL ** -0.5) * ev_scale,
        'hy_conv_w': nrm((N_EVEN, HY_SHORT, 3 * HY_DIM), HY_SHORT ** -0.5),
        'hy_conv_b': nrm((N_EVEN, 3 * HY_DIM), 0.02),
        'hy_filt_w1': nrm((N_EVEN, HY_EMB, HY_FILT_HID), HY_EMB ** -0.5),
        'hy_filt_b1': nrm((N_EVEN, HY_FILT_HID), 0.1),
        'hy_filt_w2': nrm((N_EVEN, HY_FILT_HID, HY_FILT_HID), HY_FILT_HID ** -0.5),
        'hy_filt_b2': nrm((N_EVEN, HY_FILT_HID), 0.1),
        'hy_filt_w3': nrm((N_EVEN, HY_FILT_HID, HY_FILT_HID), HY_FILT_HID ** -0.5),
        'hy_filt_b3': nrm((N_EVEN, HY_FILT_HID), 0.1),
        'hy_filt_w4': nrm((N_EVEN, HY_FILT_HID, HY_ORDER * 2 * HY_DIM), 0.1 * HY_FILT_HID ** -0.5),
        'hy_filt_freq': 1.0 + nrm((N_EVEN, HY_FILT_HID), 0.1),
        'hy_skip': nrm((N_EVEN, HY_ORDER, HY_DIM), 0.5),
        'gdn_conv_w': nrm((N_EVEN, GDN_CONV, 2 * GDN_HEADS * GDN_DK + GDN_HEADS * GDN_DV), GDN_CONV ** -0.5),
        'gdn_a_log': jnp.log(uni((N_EVEN, 2, GDN_HEADS), 1.0, 16.0)),
        'gdn_dt_bias': dt + jnp.log(-jnp.expm1(-dt)),
        'gdn_norm_w': 1.0 + nrm((N_EVEN, GDN_DV), 0.1),
        'ev_w_out': nrm((N_EVEN, MIX_WIDTH, D_MODEL), MIX_WIDTH ** -0.5 * beta),
        'od_w_in': nrm((N_ODD, D_MODEL, ODD_IN), D_MODEL ** -0.5) * od_scale,
        'hg_lower': nrm((DEPTH, HG_HEADS * HG_DK), 1.0),
        'hg_norm_w': 1.0 + nrm((N_ODD, HG_DV), 0.1),
        'rw_mu': uni((N_ODD, 2, RW_IN), 0.0, 0.5),
        'rw_w0': uni((N_ODD, 2, RW_DIM), -6.0, -1.0),
        'rw_w2': nrm((N_ODD, 2, RW_W_LORA, RW_DIM), 0.1 * RW_W_LORA ** -0.5),
        'rw_a0': nrm((N_ODD, RW_DIM), 0.1),
        'rw_a2': nrm((N_ODD, RW_A_LORA, RW_DIM), RW_A_LORA ** -0.5),
        'rw_g2': nrm((N_ODD, RW_G_LORA, RW_DIM), RW_G_LORA ** -0.5),
        'rw_k_k': 0.85 + nrm((N_ODD, RW_DIM), 0.05),
        'rw_k_a': 1.0 + nrm((N_ODD, RW_DIM), 0.05),
        'rw_r_k': nrm((N_ODD, RW_HEADS, RW_HD), 0.1),
        'rw_lnx_w': 1.0 + nrm((N_ODD, RW_DIM), 0.1),
        'rw_lnx_b': nrm((N_ODD, RW_DIM), 0.05),
        'od_w_out': nrm((N_ODD, MIX_WIDTH, D_MODEL), MIX_WIDTH ** -0.5 * beta),
        'ln_g': 1.0 + nrm((DEPTH, 2, D_MODEL), 0.05),
        'ln_b': nrm((DEPTH, 2, D_MODEL), 0.02),
        'mlp_w1': nrm((DEPTH, D_MODEL, D_FF), D_MODEL ** -0.5 * beta),
        'mlp_w2': nrm((DEPTH, D_FF, D_MODEL), D_FF ** -0.5 * beta),
    }


def reference(x, ev_w_in, hy_conv_w, hy_conv_b, hy_filt_w1, hy_filt_b1, hy_filt_w2, hy_filt_b2,
              hy_filt_w3, hy_filt_b3, hy_filt_w4, hy_filt_freq, hy_skip, gdn_conv_w, gdn_a_log,
              gdn_dt_bias, gdn_norm_w, ev_w_out, od_w_in, hg_lower, hg_norm_w, rw_mu, rw_w0, rw_w2,
              rw_a0, rw_a2, rw_g2, rw_k_k, rw_k_a, rw_r_k, rw_lnx_w, rw_lnx_b, od_w_out, ln_g, ln_b,
              mlp_w1, mlp_w2):
    lb_all = jnp.cumsum(jax.nn.softmax(hg_lower.astype(jnp.float32), axis=0), axis=0)
    lb_all = lb_all - lb_all[0]
    for layer in range(DEPTH):
        idx = layer // 2
        if layer % 2 == 0:
            mix = even_layer_mixer(x, ev_w_in[idx], hy_conv_w[idx], hy_conv_b[idx], hy_filt_w1[idx],
                                   hy_filt_b1[idx], hy_filt_w2[idx], hy_filt_b2[idx], hy_filt_w3[idx],
                                   hy_filt_b3[idx], hy_filt_w4[idx], hy_filt_freq[idx], hy_skip[idx],
                                   gdn_conv_w[idx], gdn_a_log[idx], gdn_dt_bias[idx], gdn_norm_w[idx],
                                   ev_w_out[idx])
        else:
            mix = odd_layer_mixer(x, od_w_in[idx], lb_all[layer], hg_norm_w[idx], rw_mu[idx], rw_w0[idx],
                                  rw_w2[idx], rw_a0[idx], rw_a2[idx], rw_g2[idx], rw_k_k[idx], rw_k_a[idx],
                                  rw_r_k[idx], rw_lnx_w[idx], rw_lnx_b[idx], od_w_out[idx])
        x = layer_norm(DEEPNORM_ALPHA * x + mix, ln_g[layer, 0], ln_b[layer, 0])
        x = layer_norm(DEEPNORM_ALPHA * x + sq_relu_mlp(x, mlp_w1[layer], mlp_w2[layer]),
                       ln_g[layer, 1], ln_b[layer, 1])
    return x
```

```python
import contextlib
import numpy as np
import concourse.bass as bass
import concourse.mybir as mybir
from concourse.bass_utils import run_bass_kernel_spmd

F32 = mybir.dt.float32
BF16 = mybir.dt.bfloat16
AF = mybir.ActivationFunctionType
ALU = mybir.AluOpType
AX = mybir.AxisListType

N_DMA_SEMS = 6


def _box(ap):
    name = ap.tensor.name
    dims = list(ap.ap)
    sp = str(ap.space).lower()
    if 'dram' in sp or 'hbm' in sp:
        lo = hi = int(ap.offset)
        for s, c in dims:
            d = (int(c) - 1) * int(s)
            if d < 0:
                lo += d
            else:
                hi += d
        return (name, 0, 1, lo, hi + 1)
    if 'psum' in sp:
        return (name, 0, 128, 0, 1 << 30)
    p0 = ap.start_partition
    p0 = int(p0() if callable(p0) else p0)
    pn = ap.partition_size
    pn = int(pn() if callable(pn) else pn)
    pstep = int(dims[0][0])
    off = int(ap.offset) % pstep if pstep > 0 else int(ap.offset)
    lo = hi = off
    for s, c in dims[1:]:
        d = (int(c) - 1) * int(s)
        if d < 0:
            lo += d
        else:
            hi += d
    return (name, p0, p0 + pn, lo, hi + 1)


def _ov(a, b):
    return a[1] < b[2] and b[1] < a[2] and a[3] < b[4] and b[3] < a[4]


def _contains(a, b):
    return a[1] <= b[1] and b[2] <= a[2] and a[3] <= b[3] and b[4] <= a[4]


class Prog:
    ENG = ('pe', 'act', 'dve', 'pool', 'sp')

    def __init__(self):
        self.nc = bass.Bass('TRN2', target_bir_lowering=False)
        nc = self.nc
        self.eng = {'pe': nc.tensor, 'act': nc.scalar, 'dve': nc.vector, 'pool': nc.gpsimd, 'sp': nc.sync}
        self.stack = contextlib.ExitStack()
        self.sem = {}
        self.cnt = {}
        self.known = {e: {} for e in self.ENG}
        self.track = {}
        self.dma_rr = {e: 0 for e in self.ENG}
        self.nops = 0
        self.uid = 0
        for e in ('pe', 'act', 'dve', 'pool'):
            self._mksem('c_' + e)
        for e in ('sp', 'act', 'pool'):
            for j in range(N_DMA_SEMS):
                self._mksem('d_%s%d' % (e, j))

    def _mksem(self, name):
        self.sem[name] = self.stack.enter_context(self.nc.semaphore(name))
        self.cnt[name] = 0

    def dram(self, name, shape, dt, kind):
        return self.nc.dram_tensor(name, list(shape), dt, kind=kind).ap()

    def sb(self, name, shape, dt=F32):
        return self.stack.enter_context(self.nc.sbuf_tensor(name, list(shape), dt))

    def ps(self, name, shape, dt=F32):
        return self.stack.enter_context(self.nc.psum_tensor(name, list(shape), dt))

    def _deps(self, reads, writes, self_sem):
        need = {}

        def req(sv):
            if sv is None:
                return
            s, v = sv
            if need.get(s, 0) < v:
                need[s] = v
        for ap in reads:
            b = _box(ap)
            for rec in self.track.get(b[0], ()):
                if _ov(rec[0], b):
                    req(rec[1])
        for ap in writes:
            b = _box(ap)
            for rec in self.track.get(b[0], ()):
                if _ov(rec[0], b):
                    req(rec[1])
                    for s, v in rec[2].items():
                        req((s, v))
        return need

    def _commit(self, reads, writes, sem, val):
        for ap in reads:
            b = _box(ap)
            lst = self.track.setdefault(b[0], [])
            exact = False
            for rec in lst:
                if _ov(rec[0], b):
                    rec[2][sem] = val
                    if rec[0] == b or _contains(rec[0], b):
                        exact = True
            if not exact:
                lst.append([b, None, {sem: val}])
        for ap in writes:
            b = _box(ap)
            lst = self.track.setdefault(b[0], [])
            lst[:] = [r for r in lst if not _contains(b, r[0])]
            lst.append([b, (sem, val), {}])

    def _emit(self, eng, sem, inc, fn, reads, writes, skip_self=False):
        psr = [a for a in reads if 'psum' in str(a.space).lower()]
        if psr:
            writes = list(writes) + psr
        need = self._deps(reads, writes, sem)
        e = self.eng[eng]
        kn = self.known[eng]
        for s, v in need.items():
            if skip_self and s == sem:
                continue
            if kn.get(s, 0) < v:
                e.wait_ge(self.sem[s], v)
                kn[s] = v
        inst = fn(e)
        self.cnt[sem] += inc
        inst.then_inc(self.sem[sem], inc)
        self._commit(reads, writes, sem, self.cnt[sem])
        self.nops += 1
        return inst

    def op(self, eng, fn, reads, writes):
        return self._emit(eng, 'c_' + eng, 1, fn, reads, writes, skip_self=(eng == 'pe'))

    def dma(self, out, in_, q='sp', **kw):
        j = self.dma_rr[q]
        self.dma_rr[q] = (j + 1) % N_DMA_SEMS
        sem = 'd_%s%d' % (q, j)
        return self._emit(q, sem, 16, lambda e: e.dma_start(out=out, in_=in_, **kw), [in_], [out])

    def finish(self, eng='sp'):
        e = self.eng[eng]
        for s, v in self.cnt.items():
            if v > 0:
                e.wait_ge(self.sem[s], v)

    def mm(self, out, lhsT, rhs, start=True, stop=True):
        return self.op('pe', lambda e: e.matmul(out, lhsT, rhs, start=start, stop=stop), [lhsT, rhs], [out])

    def tr(self, out, in_, ident):
        return self.op('pe', lambda e: e.matmul(out, in_, ident, start=True, stop=True), [in_, ident], [out])

    def act(self, out, in_, func, bias=None, scale=None, accum=None, eng='act'):
        kw = {}
        rd = [in_]
        wr = [out]
        if bias is not None:
            kw['bias'] = bias
            if not isinstance(bias, (int, float)):
                rd.append(bias)
        if scale is not None:
            kw['scale'] = scale
            if not isinstance(scale, (int, float)):
                rd.append(scale)
        if accum is not None:
            kw['accum_out'] = accum
            wr.append(accum)
        return self.op('act', lambda e: e.activation(out, in_, func, **kw), rd, wr)

    def tt(self, out, a, b, op, eng='dve'):
        return self.op(eng, lambda e: e.tensor_tensor(out, a, b, op), [a, b], [out])

    def ts(self, out, a, s1, op0, s2=None, op1=None, eng='dve', accum=None):
        rd = [a]
        for s in (s1, s2):
            if s is not None and not isinstance(s, (int, float)):
                rd.append(s)
        wr = [out] + ([accum] if accum is not None else [])
        kw = {}
        if op1 is not None:
            kw['op1'] = op1
        if accum is not None:
            kw['accum_out'] = accum
        return self.op(eng, lambda e: e.tensor_scalar(out, a, s1, s2, op0, **kw), rd, wr)

    def stt(self, out, a, s, b, op0, op1, accum=None):
        rd = [a, b] + ([] if isinstance(s, (int, float)) else [s])
        wr = [out] + ([accum] if accum is not None else [])
        kw = {'accum_out': accum} if accum is not None else {}
        return self.op('dve', lambda e: e.scalar_tensor_tensor(out, a, s, b, op0, op1, **kw), rd, wr)

    def cp(self, out, in_, eng='dve'):
        if eng == 'act':
            return self.op('act', lambda e: e.copy(out, in_), [in_], [out])
        return self.op(eng, lambda e: e.tensor_copy(out, in_), [in_], [out])

    def memset(self, ap, val, eng='dve'):
        return self.op(eng, lambda e: e.memset(ap, val), [], [ap])

    def close(self):
        self.stack.close()

    @contextlib.contextmanager
    def scope(self):
        outer = self.stack
        self.stack = contextlib.ExitStack()
        try:
            yield
        finally:
            self.barrier()
            self.stack.close()
            self.stack = outer

    def barrier(self):
        for en in ('pe', 'act', 'dve', 'pool', 'sp'):
            e = self.eng[en]
            kn = self.known[en]
            for sname, v in self.cnt.items():
                if v > 0 and kn.get(sname, 0) < v:
                    e.wait_ge(self.sem[sname], v)
                    kn[sname] = v

    def red(self, out, in_, op=None, eng='dve'):
        op = op or ALU.add
        return self.op(eng, lambda e: e.tensor_reduce(out, in_, AX.X, op), [in_], [out])

    def recip(self, out, in_):
        return self.op('dve', lambda e: e.reciprocal(out, in_), [in_], [out])

    def rsqrt(self, out, in_, eps):
        self.ts(out, in_, float(eps), ALU.add)
        self.act(out, out, AF.Sqrt)
        self.recip(out, out)

    def wrap(self, out, in_, m1, m2):
        self.ts(m1, in_, -float(np.pi), ALU.is_lt)
        self.ts(m2, in_, float(np.pi), ALU.is_gt)
        self.stt(out, m1, float(2 * np.pi), in_, ALU.mult, ALU.add)
        self.stt(out, m2, -float(2 * np.pi), out, ALU.mult, ALU.add)

    def bc(self, name, vec, n):
        t = self.sb(name, [128, n])
        self.dma(t[:], vec.rearrange('(o f) -> o f', o=1).broadcast_to([128, n]), q='act')
        return t


CH = 32
NCH = 128 // CH


def recur_consts():
    t = np.arange(128)
    same = (t[:, None] // CH) == (t[None, :] // CH)
    ident = np.eye(128, dtype=np.float32)
    tri = (same & (t[:, None] <= t[None, :])).astype(np.float32)
    blk = same.astype(np.float32)
    msl = (same & (t[None, :] < t[:, None])).astype(np.float32)
    msu = (same & (t[:, None] < t[None, :])).astype(np.float32)
    return np.concatenate([ident, tri, blk, msl, msu, msl, tri, -tri], axis=1).astype(np.float32)


class Recur:
    def __init__(self, p, cdram, dk, dv, delta, tag=''):
        self.p, self.dk, self.dv, self.delta = p, dk, dv, delta
        g = tag
        self.c = p.sb(g + 'rc', [128, 8 * 128])
        p.dma(self.c[:], cdram)
        c = self.c
        self.ident, self.tri, self.blk = c[:, 0:128], c[:, 128:256], c[:, 256:384]
        self.msl, self.msu = c[:, 384:512], c[:, 512:640]
        self.mcat = c[:, 384:896]
        self.ntri = c[:, 896:1024]
        sb, ps = p.sb, p.ps
        nin = 6 if delta else 4
        self.inp = [sb(g + 'in%d' % i, [128, nin, 128]) for i in range(2)]
        self.gs = sb(g + 'gs', [128, 6, 128])
        self.ex = sb(g + 'ex', [128, 5, 128])
        self.sc = sb(g + 'sc', [128, 4, 128])
        self.kd = sb(g + 'kd', [128, 128])
        self.kad = sb(g + 'kad', [128, 128])
        self.vb = sb(g + 'vb', [128, 128])
        self.fT = sb(g + 'fT', [128, 4, 128])
        self.elc = sb(g + 'elc', [128, NCH])
        self.sS = sb(g + 'sS', [128, 3, 128])
        self.arkT = sb(g + 'arkT', [128, 128])
        self.araT = sb(g + 'araT', [128, 128])
        self.nm = sb(g + 'nm', [128, 24, 128])
        self.wmT = sb(g + 'wmT', [128, 128])
        self.tbT = sb(g + 'tbT', [128, 128])
        self.u0 = sb(g + 'u0', [128, 128])
        self.ub = sb(g + 'ub', [128, 128])
        self.ys = sb(g + 'ys', [128, 128])
        self.yo = [sb(g + 'yo%d' % i, [128, 128]) for i in range(2)]
        self.M = sb(g + 'M', [128, 128])
        self.Mb = sb(g + 'Mb', [128, 128])
        self.pb = [ps(g + 'pb%d' % i, [128, 512]) for i in range(8)]

    def reset(self):
        p = self.p
        if not getattr(self, '_padded', False):
            p.memset(self.fT[:].rearrange('p a b -> p (a b)'), 0.0)
            self._padded = True
        p.memset(self.M[:], 0.0)
        p.memset(self.Mb[:], 0.0, eng='pool')

    def tile(self, it, srcs, ydst):
        p, dk, dv, delta = self.p, self.dk, self.dv, self.delta
        pb = self.pb
        X = self.inp[it % 2]
        qs = ['sp', 'act']
        for i, s in enumerate(srcs):
            d = dv if i == 2 else dk
            p.dma(X[:, i, 0:d], s, q=qs[i % 2])
        r, k, v, lw = X[:, 0, 0:dk], X[:, 1, 0:dk], X[:, 2, 0:dv], X[:, 3, 0:dk]
        gs, ex, sc = self.gs, self.ex, self.sc
        p.mm(pb[0][:, 0:dk], self.tri, lw)
        p.mm(pb[0][:, 128:128 + dk], self.blk, lw)
        gam, gl = pb[0][:, 0:dk], pb[0][:, 128:128 + dk]
        G, NG, GX, GD = gs[:, 0, 0:dk], gs[:, 1, 0:dk], gs[:, 2, 0:dk], gs[:, 3, 0:dk]
        E, Ei, Ex, Ed, EL = (ex[:, i, 0:dk] for i in range(5))
        p.cp(G, gam, eng='act')
        p.act(EL, gl, AF.Exp)
        p.tt(GD, gl, G, ALU.subtract)
        p.act(E, G, AF.Exp)
        p.ts(NG, G, -80.0, ALU.max)
        p.act(Ei, NG, AF.Exp, scale=-1.0)
        p.act(Ed, GD, AF.Exp)
        rt, kh, kx, kah = (sc[:, i, 0:dk] for i in range(4))
        p.tt(rt, r, E, ALU.mult)
        p.tt(kh, k, Ei, ALU.mult, eng='pool')
        p.tt(self.kd[:, 0:dk], k, Ed, ALU.mult)
        p.cp(self.vb[:, 0:dv], v, eng='pool')
        nT = 2
        if delta:
            kk, ka = X[:, 4, 0:dk], X[:, 5, 0:dk]
            p.tt(GX, G, lw, ALU.subtract, eng='pool')
            p.act(Ex, GX, AF.Exp)
            p.tt(kx, kk, Ex, ALU.mult)
            p.tt(kah, ka, Ei, ALU.mult, eng='pool')
            p.stt(self.kad[:, 0:dk], ka, -1.0, Ed, ALU.mult, ALU.mult)
            nT = 4
        for i in range(nT):
            p.tr(pb[1][0:dk, i * 128:(i + 1) * 128], sc[:, i, 0:dk], self.ident)
        p.tr(pb[0][0:dk, 256:384], EL, self.ident)
        p.cp(self.fT[0:dk, 0:nT, :], pb[1][0:dk, 0:nT * 128].rearrange('p (a b) -> p a b', a=nT), eng='act')
        p.cp(self.elc[0:dk, :], pb[0][0:dk, 256:384].rearrange('p (a b) -> p a b', a=NCH)[:, :, 0], eng='dve')
        rT, khT, kxT, kahT = (self.fT[:, i, :] for i in range(4))
        S = pb[2]
        if delta:
            p.mm(S[:, 0:128], kxT, kahT)
            p.mm(S[:, 128:256], kahT, kxT)
            p.mm(S[:, 256:384], kxT, khT)
            p.mm(S[:, 384:512], khT, rT)
            p.mm(pb[3][:, 0:128], kahT, rT)
            p.tt(self.sS[:].rearrange('p a b -> p (a b)'), S[:, 0:384], self.mcat[:, 0:384], ALU.mult)
            p.tt(self.arkT[:], S[:, 384:512], self.tri, ALU.mult)
            p.tt(self.araT[:], pb[3][:, 0:128], self.ntri, ALU.mult)
            TT = self._solve()
            p.mm(pb[3][0:dk, 128:256], kx, TT)
            p.cp(self.wmT[0:dk, :], pb[3][0:dk, 128:256], eng='act')
            p.mm(pb[3][:, 256:384], self.sS[:, 2, :], TT)
            p.cp(self.tbT[:], pb[3][:, 256:384], eng='dve')
            p.mm(pb[3][:, 384:384 + dv], self.tbT[:], self.vb[:, 0:dv])
            p.cp(self.u0[:, 0:dv], pb[3][:, 384:384 + dv], eng='act')
        else:
            p.mm(S[:, 384:512], khT, rT)
            p.tt(self.arkT[:], S[:, 384:512], self.tri, ALU.mult)
        M, Mb = self.M, self.Mb
        for n in range(NCH):
            sl = slice(CH * n, CH * (n + 1))
            tp0 = (0, CH * n)
            tpk = (CH * n, 0)
            p.op('pe', lambda e, sl=sl, tp0=tp0: e.matmul(pb[4][sl, 0:dv], rT[0:dk, sl], Mb[0:dk, 0:dv], start=True, stop=True, tile_position=tp0),
                 [rT[0:dk, sl], Mb[0:dk, 0:dv]], [pb[4][sl, 0:dv]])
            p.cp(self.ys[sl, 0:dv], pb[4][sl, 0:dv], eng='act')
            if delta:
                p.op('pe', lambda e, sl=sl, tp0=tp0: e.matmul(pb[5][sl, 0:dv], self.wmT[0:dk, sl], Mb[0:dk, 0:dv], start=True, stop=True, tile_position=tp0),
                     [self.wmT[0:dk, sl], Mb[0:dk, 0:dv]], [pb[5][sl, 0:dv]])
                p.tt(self.ub[sl, 0:dv], self.u0[sl, 0:dv], pb[5][sl, 0:dv], ALU.add)
            p.op('pe', lambda e, sl=sl, tpk=tpk: e.matmul(pb[6][0:dk, 0:dv], self.kd[sl, 0:dk], self.vb[sl, 0:dv], start=True, stop=not delta, tile_position=tpk),
                 [self.kd[sl, 0:dk], self.vb[sl, 0:dv]], [pb[6][0:dk, 0:dv]])
            if delta:
                p.op('pe', lambda e, sl=sl, tpk=tpk: e.matmul(pb[6][0:dk, 0:dv], self.kad[sl, 0:dk], self.ub[sl, 0:dv], start=False, stop=True, tile_position=tpk),
                     [self.kad[sl, 0:dk], self.ub[sl, 0:dv]], [pb[6][0:dk, 0:dv]])
            p.stt(M[0:dk, 0:dv], M[0:dk, 0:dv], self.elc[0:dk, n:n + 1], pb[6][0:dk, 0:dv], ALU.mult, ALU.add)
            p.cp(Mb[0:dk, 0:dv], M[0:dk, 0:dv], eng='act')
        p.mm(pb[7][:, 0:dv], self.arkT[:], self.vb[:, 0:dv], start=True, stop=not delta)
        if delta:
            p.mm(pb[7][:, 0:dv], self.araT[:], self.ub[:, 0:dv], start=False, stop=True)
        yo = self.yo[it % 2]
        p.tt(yo[:, 0:dv], self.ys[:, 0:dv], pb[7][:, 0:dv], ALU.add)
        p.dma(ydst, yo[:, 0:dv], q='pool')

    def _solve(self):
        p = self.p
        nm, pb, I = self.nm, self.pb, self.ident
        A, AT = self.sS[:, 0, :], self.sS[:, 1, :]
        F = [(nm[:, 0, :], nm[:, 1, :])]
        p.tt(F[0][0], I, A, ALU.subtract)
        p.tt(F[0][1], I, AT, ALU.subtract, eng='pool')
        X, XT = A, AT
        slot = 2
        for lev in range(4):
            P0, P1 = pb[3][:, 0:128], pb[3][:, 128:256]
            p.mm(P0, XT, X)
            p.mm(P1, X, XT)
            f, fT = nm[:, slot, :], nm[:, slot + 1, :]
            p.tt(f, P0, I, ALU.add)
            p.tt(fT, P1, I, ALU.add)
            F.append((f, fT))
            if lev < 3:
                X, XT = nm[:, slot + 2, :], nm[:, slot + 3, :]
                p.cp(X, P0, eng='act')
                p.cp(XT, P1, eng='act')
            slot += 4

        def prod(dst, lhsT, rhs, bank):
            p.mm(bank, lhsT, rhs)
            p.cp(dst, bank, eng='act')
            return dst
        s = 18
        G1 = prod(nm[:, s, :], F[0][0], F[1][1], pb[3][:, 0:128])
        H1 = prod(nm[:, s + 1, :], F[0][1], F[1][0], pb[3][:, 128:256])
        G2 = prod(nm[:, s + 2, :], F[2][0], F[3][1], pb[3][:, 256:384])
        H2 = prod(nm[:, s + 3, :], F[2][1], F[3][0], pb[3][:, 384:512])
        G3 = prod(nm[:, s + 4, :], H1, G2, pb[3][:, 0:128])
        H3 = prod(nm[:, s + 5, :], G1, H2, pb[3][:, 128:256])
        TT = prod(nm[:, 17, :], H3, F[4][1], pb[3][:, 256:384])
        return TT


LN_EPS = 1e-5
D = 1024
ALPHA = 4.0 ** 0.25


def _h3(ap, H):
    return ap.rearrange('p (h d) -> p h d', h=H)


def _bcol(col, H, d):
    return col.unsqueeze(2).broadcast_to([128, H, d])


def linear_program(p, ident, x, w, y, ntok, K, N, relu2=False, tag='l', f32=False):
    kc = K // 128
    DT = F32 if f32 else BF16
    with p.scope():
        wb = p.sb(tag + 'wb', [128, kc, N], DT)
        wst = [p.sb(tag + 'wst%d' % i, [128, N]) for i in range(2)]
        for c in range(kc):
            if f32:
                p.dma(wb[:, c, :], w[c * 128:(c + 1) * 128, :], q=('sp', 'act')[c % 2])
            else:
                p.dma(wst[c % 2][:], w[c * 128:(c + 1) * 128, :], q=('sp', 'act')[c % 2])
                p.cp(wb[:, c, :], wst[c % 2][:], eng=('dve', 'pool')[c % 2])
        xs = [p.sb(tag + 'xs%d' % i, [128, K]) for i in range(2)]
        xT = [p.sb(tag + 'xT%d' % i, [128, kc, 128], DT) for i in range(2)]
        ys = [p.sb(tag + 'ys%d' % i, [128, 512]) for i in range(2)]
        pt = [p.ps(tag + 'pt%d' % i, [128, 512]) for i in range(2)]
        pa = [p.ps(tag + 'pa%d' % i, [128, 512]) for i in range(4)]
        nb = 0
        for t in range(ntok // 128):
            X = xs[t % 2]
            p.dma(X[:], x[t * 128:(t + 1) * 128, :], q='sp')
            for g in range(0, kc, 4):
                n4 = min(4, kc - g)
                P = pt[(g // 4) % 2]
                for j in range(n4):
                    src = X[:, (g + j) * 128:(g + j + 1) * 128]
                    dst = P[:, j * 128:(j + 1) * 128]
                    p.op('pe', lambda e, dst=dst, src=src: e.transpose(dst, src, ident), [src, ident], [dst])
                p.cp(xT[t % 2][:, g:g + n4, :], P[:, 0:n4 * 128].rearrange('p (a b) -> p a b', a=n4), eng='act')
            for n0 in range(0, N, 512):
                nn = min(512, N - n0)
                A = pa[nb % 4]
                Y = ys[nb % 2]
                for c in range(kc):
                    p.mm(A[:, 0:nn], xT[t % 2][:, c, :], wb[:, c, n0:n0 + nn], start=(c == 0), stop=(c == kc - 1))
                if relu2:
                    p.act(Y[:, 0:nn], A[:, 0:nn], AF.Relu)
                    p.tt(Y[:, 0:nn], Y[:, 0:nn], Y[:, 0:nn], ALU.mult, eng='pool')
                else:
                    p.cp(Y[:, 0:nn], A[:, 0:nn], eng=('act', 'dve')[nb % 2])
                p.dma(y[t * 128:(t + 1) * 128, n0:n0 + nn], Y[:, 0:nn], q='pool')
                nb += 1


def ln_program(p, xa, xb, gv, bv, out, ntok, alpha, tag='n'):
    with p.scope():
        g = p.bc(tag + 'g', gv, D)
        b = p.bc(tag + 'b', bv, D)
        A = [p.sb(tag + 'a%d' % i, [128, D]) for i in range(2)]
        B = [p.sb(tag + 'b%d' % i, [128, D]) for i in range(2)]
        st = p.sb(tag + 'st', [128, 12])
        mv = p.sb(tag + 'mv', [128, 2])
        rs = p.sb(tag + 'rs', [128, 1])
        for t in range(ntok // 128):
            rows = slice(t * 128, (t + 1) * 128)
            a, bb = A[t % 2], B[t % 2]
            p.dma(a[:], xa[rows, :])
            p.dma(bb[:], xb[rows, :], q='act')
            p.stt(a[:], a[:], float(alpha), bb[:], ALU.mult, ALU.add)
            p.op('dve', lambda e, a=a: e.bn_stats(st[:, 0:6], a[:, 0:512]), [a[:, 0:512]], [st[:, 0:6]])
            p.op('dve', lambda e, a=a: e.bn_stats(st[:, 6:12], a[:, 512:1024]), [a[:, 512:1024]], [st[:, 6:12]])
            p.op('dve', lambda e: e.bn_aggr(mv[:], st[:]), [st[:]], [mv[:]])
            p.rsqrt(rs[:], mv[:, 1:2], LN_EPS)
            p.ts(a[:], a[:], mv[:, 0:1], ALU.subtract, rs[:], ALU.mult)
            p.tt(a[:], a[:], g[:], ALU.mult)
            p.tt(a[:], a[:], b[:], ALU.add, eng='pool')
            p.dma(out[rows, :], a[:], q='pool')


def head_norm(p, out, x, H, d, eps, tmp, stat, center=False):
    x3, o3, t3 = _h3(x, H), _h3(out, H), _h3(tmp, H)
    mean, ms = stat[:, 0:H], stat[:, H:2 * H]
    if center:
        p.red(mean, x3)
        p.ts(mean, mean, 1.0 / d, ALU.mult)
        p.tt(o3, x3, _bcol(mean, H, d), ALU.subtract)
        src = o3
    else:
        src = x3
    p.tt(t3, src, src, ALU.mult)
    p.red(ms, t3)
    p.ts(ms, ms, 1.0 / d, ALU.mult)
    p.rsqrt(ms, ms, eps)
    p.tt(o3, src, _bcol(ms, H, d), ALU.mult)


def prep_odd(p, ident, projh, prm, outs, ntok):
    with p.scope():
        l0 = p.bc('pl0', prm['hg_lower'][0, :], 512)
        lb = p.bc('plb', prm['hg_lower'][1, :], 512)
        p.tt(lb[:], lb[:], l0[:], ALU.subtract)
        p.act(lb[:], lb[:], AF.Sigmoid)
        oml = p.sb('poml', [128, 512])
        p.ts(oml[:], lb[:], -1.0, ALU.mult, 1.0, ALU.add)
        mu0 = p.bc('pmu0', prm['rw_mu'][0, :], 1792)
        mu1 = p.bc('pmu1', prm['rw_mu'][1, :], 1792)
        w0 = [p.bc('pw0%d' % d, prm['rw_w0'][d, :], 512) for d in range(2)]
        a0 = p.bc('pa0', prm['rw_a0'], 512)
        k_k = p.bc('pkk', prm['rw_k_k'], 512)
        k_a = p.bc('pka', prm['rw_k_a'], 512)
        w2 = [p.sb('pw2%d' % d, [64, 512]) for d in range(2)]
        a2 = p.sb('pa2', [64, 512])
        g2 = p.sb('pg2', [128, 512])
        for d in range(2):
            p.dma(w2[d][:], prm['rw_w2'][d, :, :])
        p.dma(a2[:], prm['rw_a2'])
        p.dma(g2[:], prm['rw_g2'])
        hin = [p.sb('phin%d' % i, [128, 2560]) for i in range(2)]
        cur = [p.sb('pcur%d' % i, [128, 1792]) for i in range(2)]
        prv = [p.sb('pprv%d' % i, [128, 1792]) for i in range(2)]
        nxt = [p.sb('pnxt%d' % i, [128, 1792]) for i in range(2)]
        hq = p.sb('phq', [128, 512])
        hf = p.sb('phf', [128, 1024])
        hk = p.sb('phk', [128, 1024])
        hlw = p.sb('phlw', [128, 1024])
        lo = p.sb('plo', [128, 256])
        loT = p.sb('ploT', [128, 3, 128])
        rlw = p.sb('prlw', [128, 1024])
        ra = p.sb('pra', [128, 512])
        rg = p.sb('prg', [128, 512])
        kq = p.sb('pkq', [128, 512])
        kkn = p.sb('pkkn', [128, 512])
        tmp = p.sb('ptmp', [128, 512])
        kp = p.sb('pkp', [128, 512])
        kav = p.sb('pkav', [128, 512])
        stat = p.sb('pstat', [128, 16])
        pp = [p.ps('ppp%d' % i, [128, 512]) for i in range(4)]
        for t in range(ntok // 128):
            r0 = t * 128
            rows = slice(r0, r0 + 128)
            H_, C_, P_, N_ = hin[t % 2], cur[t % 2], prv[t % 2], nxt[t % 2]
            p.dma(H_[:], projh[r0 + 1:r0 + 129, 0:2560])
            p.dma(C_[:], projh[r0 + 1:r0 + 129, 2560:4352], q='act')
            p.dma(P_[:], projh[r0:r0 + 128, 2560:4352])
            p.dma(N_[:], projh[r0 + 2:r0 + 130, 2560:4352], q='act')
            p.act(hq[:], H_[:, 0:512], AF.Silu)
            p.dma(outs['hq'][rows, :], hq[:], q='pool')
            p.act(hf[:], H_[:, 512:1536], AF.Sigmoid)
            for d in range(2):
                sl = slice(512 * d, 512 * (d + 1))
                p.tt(hf[:, sl], hf[:, sl], oml[:], ALU.mult)
                p.tt(hf[:, sl], hf[:, sl], lb[:], ALU.add, eng='pool')
            p.ts(hk[:], hf[:], -1.0, ALU.mult, 1.0, ALU.add)
            p.act(hlw[:], hf[:], AF.Ln)
            p.dma(outs['hk'][rows, :], hk[:], q='pool')
            p.dma(outs['hlw'][rows, :], hlw[:], q='pool')
            p.tt(P_[:], P_[:], C_[:], ALU.subtract)
            p.tt(P_[:], P_[:], mu0[:], ALU.mult)
            p.tt(N_[:], N_[:], C_[:], ALU.subtract, eng='pool')
            p.tt(N_[:], N_[:], mu1[:], ALU.mult, eng='pool')
            p.tt(C_[:], C_[:], P_[:], ALU.add)
            p.tt(C_[:], C_[:], N_[:], ALU.add)
            r, k, v = C_[:, 0:512], C_[:, 512:1024], C_[:, 1024:1536]
            p.dma(outs['rr'][rows, :], r, q='pool')
            p.dma(outs['rv'][rows, :], v, q='pool')
            p.act(lo[:, 0:64], C_[:, 1536:1600], AF.Tanh)
            p.cp(lo[:, 64:128], C_[:, 1600:1664])
            p.act(lo[:, 128:256], C_[:, 1664:1792], AF.Sigmoid)
            p.tr(pp[0][0:64, 0:128], lo[:, 0:64], ident)
            p.tr(pp[0][0:64, 128:256], lo[:, 64:128], ident)
            p.tr(pp[0][:, 256:384], lo[:, 128:256], ident)
            p.cp(loT[0:64, 0:2, :], pp[0][0:64, 0:256].rearrange('p (a b) -> p a b', a=2), eng='act')
            p.cp(loT[:, 2, :], pp[0][:, 256:384], eng='act')
            for d in range(2):
                p.mm(pp[1 + d][:, :], loT[0:64, 0, :], w2[d][:])
                sl = slice(512 * d, 512 * (d + 1))
                p.tt(rlw[:, sl], pp[1 + d][:, :], w0[d][:], ALU.add)
            p.act(rlw[:], rlw[:], AF.Sigmoid)
            p.ts(rlw[:], rlw[:], -float(np.exp(-0.5)), ALU.mult)
            p.dma(outs['rlw'][rows, :], rlw[:], q='pool')
            p.mm(pp[3][:, :], loT[0:64, 1, :], a2[:])
            p.tt(ra[:], pp[3][:, :], a0[:], ALU.add)
            p.act(ra[:], ra[:], AF.Sigmoid)
            p.mm(pp[1][:, :], loT[:, 2, :], g2[:])
            p.cp(rg[:], pp[1][:, :], eng='act')
            p.dma(outs['rg'][rows, :], rg[:], q='pool')
            p.tt(kq[:], k, k_k[:], ALU.mult)
            p.tt(_h3(tmp[:], 8), _h3(kq[:], 8), _h3(kq[:], 8), ALU.mult)
            p.red(stat[:, 0:8], _h3(tmp[:], 8))
            p.rsqrt(stat[:, 0:8], stat[:, 0:8], 1e-6)
            p.tt(_h3(kkn[:], 8), _h3(kq[:], 8), _bcol(stat[:, 0:8], 8, 64), ALU.mult)
            p.stt(tmp[:], ra[:], -1.0, k_a[:], ALU.add, ALU.mult)
            p.ts(tmp[:], tmp[:], 1.0, ALU.add)
            p.tt(kp[:], k, tmp[:], ALU.mult)
            p.tt(kav[:], kkn[:], ra[:], ALU.mult, eng='pool')
            p.dma(outs['rk'][rows, :], kp[:], q='pool')
            p.dma(outs['rkk'][rows, :], kkn[:], q='pool')
            p.dma(outs['rka'][rows, :], kav[:], q='pool')


def post_odd(p, ins, prm, ymix, ntok):
    with p.scope():
        nw = p.sb('qnw', [128, 512])
        for h in range(4):
            p.dma(nw[:, 128 * h:128 * (h + 1)], prm['hg_norm_w'].rearrange('(o f) -> o f', o=1).broadcast_to([128, 128]), q='act')
        lw_ = p.bc('qlw', prm['rw_lnx_w'], 512)
        lb_ = p.bc('qlb', prm['rw_lnx_b'], 512)
        rk_ = p.bc('qrk', prm['rw_r_k'], 512)
        names = ['of', 'ob', 'hg', 'yf', 'yb', 'rr', 'rk', 'rv', 'rg']
        X = [{n: p.sb('q%s%d' % (n, i), [128, 512]) for n in names} for i in range(2)]
        tmp = p.sb('qtmp', [128, 512])
        o = p.sb('qo', [128, 512])
        y = p.sb('qy', [128, 512])
        stat = p.sb('qstat', [128, 16])
        yo = [p.sb('qyo%d' % i, [128, 1024]) for i in range(2)]
        for t in range(ntok // 128):
            rows = slice(t * 128, (t + 1) * 128)
            x = X[t % 2]
            Y = yo[t % 2]
            for i, n in enumerate(names):
                p.dma(x[n][:], ins[n][rows, :], q=('sp', 'act')[i % 2])
            p.tt(o[:], x['of'][:], x['ob'][:], ALU.add)
            head_norm(p, o[:], o[:], 4, 128, 1e-6, tmp[:], stat[:])
            p.tt(o[:], o[:], nw[:], ALU.mult)
            p.act(x['hg'][:], x['hg'][:], AF.Silu)
            p.tt(Y[:, 0:512], o[:], x['hg'][:], ALU.mult)
            p.tt(y[:], x['yf'][:], x['yb'][:], ALU.add)
            head_norm(p, y[:], y[:], 8, 64, 64e-5, tmp[:], stat[:], center=True)
            p.tt(y[:], y[:], lw_[:], ALU.mult)
            p.tt(y[:], y[:], lb_[:], ALU.add, eng='pool')
            p.tt(tmp[:], x['rr'][:], x['rk'][:], ALU.mult)
            p.tt(tmp[:], tmp[:], rk_[:], ALU.mult)
            p.red(stat[:, 0:8], _h3(tmp[:], 8))
            p.tt(_h3(tmp[:], 8), _h3(x['rv'][:], 8), _bcol(stat[:, 0:8], 8, 64), ALU.mult)
            p.tt(y[:], y[:], tmp[:], ALU.add)
            p.tt(Y[:, 512:1024], y[:], x['rg'][:], ALU.mult)
            p.dma(ymix[rows, :], Y[:], q='pool')


def f32c(a):
    return np.ascontiguousarray(np.asarray(a, dtype=np.float32))


def launch(body, in_maps, out_shapes, internal=None):
    n = len(in_maps)
    p = Prog()
    ins = {k: p.dram(k, v.shape, F32, 'ExternalInput') for k, v in in_maps[0].items()}
    outs = {k: p.dram(k, shp, F32, 'ExternalOutput') for k, shp in out_shapes.items()}
    scr = {k: p.dram(k, shp, F32, 'Internal') for k, shp in (internal or {}).items()}
    body(p, ins, outs, scr)
    p.barrier()
    p.close()
    res = run_bass_kernel_spmd(p.nc, in_maps, core_ids=list(range(n)))
    return res.results


def load_ident(p, ins):
    ident = p.sb('identc', [128, 128])
    p.dma(ident[:], ins['ident'])
    return ident[:]


def shard_halo(x2, B, T, n, halo=1, pad_to=128):
    ntok = B * T // n
    out = []
    for c in range(n):
        lo = c * ntok
        a = np.zeros((ntok + pad_to, x2.shape[1]), np.float32)
        a[halo:halo + ntok] = x2[lo:lo + ntok]
        for h in range(1, halo + 1):
            if (lo - h) // T == lo // T and lo - h >= 0:
                a[halo - h] = x2[lo - h]
            hi = lo + ntok - 1 + h
            if hi < B * T and hi // T == (lo + ntok - 1) // T:
                a[halo + ntok - 1 + h] = x2[hi]
        out.append(a)
    return out


def run_recur(units_a, cfg_a, units_b, cfg_b, T, n):
    consts = recur_consts()
    na, nb = len(units_a) // n, len(units_b) // n
    in_maps = []
    for c in range(n):
        m = {'rconst': consts}
        if na:
            m['ua'] = f32c(np.stack(units_a[c * na:(c + 1) * na]))
        if nb:
            m['ub'] = f32c(np.stack(units_b[c * nb:(c + 1) * nb]))
        in_maps.append(m)
    outsh = {}
    if na:
        outsh['ya'] = (na, T, cfg_a[1])
    if nb:
        outsh['yb'] = (nb, T, cfg_b[1])

    def body(p, ins, outs, scr):
        for key, cnt, cfg, yk in (('ua', na, cfg_a, 'ya'), ('ub', nb, cfg_b, 'yb')):
            if not cnt:
                continue
            dk, dv, delta = cfg
            nin = 6 if delta else 4
            with p.scope():
                R = Recur(p, ins['rconst'], dk, dv, delta, tag=key)
                for u in range(cnt):
                    R.reset()
                    for it in range(T // 128):
                        rows = slice(it * 128, (it + 1) * 128)
                        srcs = [ins[key][u, i, rows, 0:(dv if i == 2 else dk)] for i in range(nin)]
                        R.tile(it, srcs, outs[yk][u, rows, :])
    res = launch(body, in_maps, outsh)
    ya = [res[c]['ya'][u] for c in range(n) for u in range(na)] if na else []
    yb = [res[c]['yb'][u] for c in range(n) for u in range(nb)] if nb else []
    return ya, yb


def dense_tail(p, ident, x, ymix, w, outx, scr, ntok, li):
    linear_program(p, ident, ymix, w['w_out'], scr['mix'], ntok, D, D, tag='lo')
    ln_program(p, x, scr['mix'], w['ln_g'][0, :], w['ln_b'][0, :], scr['x1'], ntok, ALPHA, tag='na')
    linear_program(p, ident, scr['x1'], w['mlp_w1'], scr['hmid'], ntok, D, 4 * D, relu2=True, tag='l1')
    linear_program(p, ident, scr['hmid'], w['mlp_w2'], scr['mlp'], ntok, 4 * D, D, tag='l2')
    ln_program(p, scr['x1'], scr['mlp'], w['ln_g'][1, :], w['ln_b'][1, :], outx, ntok, ALPHA, tag='nb')


def layer_odd(x2, W, B, T, n):
    ntok = B * T // n
    ident = np.eye(128, dtype=np.float32)
    xh = shard_halo(x2, B, T, n)
    pk = ['hg_lower', 'rw_mu', 'rw_w0', 'rw_w2', 'rw_a0', 'rw_a2', 'rw_g2', 'rw_k_k', 'rw_k_a']
    in_maps = [dict({'xh': xh[c], 'w_in': f32c(W['w_in']), 'ident': ident}, **{k: f32c(W[k]) for k in pk}) for c in range(n)]
    onames = {'hq': 512, 'hk': 1024, 'hlw': 1024, 'hv': 512, 'hg': 512, 'rr': 512, 'rk': 512, 'rv': 512,
              'rkk': 512, 'rka': 512, 'rlw': 1024, 'rg': 512}

    def body1(p, ins, outs, scr):
        idn = load_ident(p, ins)
        linear_program(p, idn, ins['xh'], ins['w_in'], scr['projh'], ntok + 128, D, 4352, tag='li')
        p.dma(outs['hv'], scr['projh'][1:ntok + 1, 1536:2048])
        p.dma(outs['hg'], scr['projh'][1:ntok + 1, 2048:2560], q='act')
        prep_odd(p, idn, scr['projh'], ins, outs, ntok)
    r1 = launch(body1, in_maps, {k: (ntok, v) for k, v in onames.items()}, {'projh': (ntok + 128, 4352)})
    full = {k: np.concatenate([r1[c][k] for c in range(n)], 0).reshape(B, T, -1) for k in onames}

    def dirv(a, z):
        return a[:, ::-1] if z else a
    ua, ub = [], []
    for b in range(B):
        for z in range(2):
            for h in range(4):
                hs = slice(128 * h, 128 * (h + 1))
                zs = slice(512 * z + 128 * h, 512 * z + 128 * (h + 1))
                arrs = [full['hq'][b:b + 1, :, hs], full['hk'][b:b + 1, :, zs], full['hv'][b:b + 1, :, hs], full['hlw'][b:b + 1, :, zs]]
                ua.append(np.stack([dirv(a, z)[0] for a in arrs]))
            for h in range(8):
                hs = slice(64 * h, 64 * (h + 1))
                zs = slice(512 * z + 64 * h, 512 * z + 64 * (h + 1))
                arrs = [full['rr'][b:b + 1, :, hs], full['rk'][b:b + 1, :, hs], full['rv'][b:b + 1, :, hs],
                        full['rlw'][b:b + 1, :, zs], full['rkk'][b:b + 1, :, hs], full['rka'][b:b + 1, :, hs]]
                ub.append(np.stack([dirv(a, z)[0] for a in arrs]))
    ya, yb = run_recur(ua, (128, 128, False), ub, (64, 64, True), T, n)
    o = np.zeros((2, B, T, 512), np.float32)
    y = np.zeros((2, B, T, 512), np.float32)
    ia = ib = 0
    for b in range(B):
        for z in range(2):
            for h in range(4):
                o[z, b, :, 128 * h:128 * (h + 1)] = ya[ia][::-1] if z else ya[ia]
                ia += 1
            for h in range(8):
                y[z, b, :, 64 * h:64 * (h + 1)] = yb[ib][::-1] if z else yb[ib]
                ib += 1

    def sh(a):
        return np.split(f32c(a.reshape(B * T, -1)), n, 0)
    parts = {'x': sh(x2), 'of': sh(o[0]), 'ob': sh(o[1]), 'yf': sh(y[0]), 'yb': sh(y[1])}
    for k in ('hg', 'rr', 'rk', 'rv', 'rg'):
        parts[k] = sh(full[k])
    wk = ['w_out', 'ln_g', 'ln_b', 'mlp_w1', 'mlp_w2', 'hg_norm_w', 'rw_lnx_w', 'rw_lnx_b', 'rw_r_k']
    in_maps = [dict({k: f32c(v[c]) for k, v in parts.items()}, ident=ident, **{k: f32c(W[k]) for k in wk}) for c in range(n)]

    def body3(p, ins, outs, scr):
        idn = load_ident(p, ins)
        post_odd(p, ins, ins, scr['ymix'], ntok)
        dense_tail(p, idn, ins['x'], scr['ymix'], ins, outs['xo'], scr, ntok, 1)
    r3 = launch(body3, in_maps, {'xo': (ntok, D)},
                {'ymix': (ntok, D), 'mix': (ntok, D), 'x1': (ntok, D), 'hmid': (ntok, 4 * D), 'mlp': (ntok, D)})
    return np.concatenate([r3[c]['xo'] for c in range(n)], 0)


def prep_even(p, projh, prm, outs, ntok):
    with p.scope():
        hw = [p.bc('ehw%d' % j, prm['hy_conv_w'][j, :], 1536) for j in range(3)]
        hbias = p.bc('ehb', prm['hy_conv_b'], 1536)
        gw = [p.bc('egw%d' % j, prm['gdn_conv_w'][j, :], 1536) for j in range(5)]
        nA = p.bc('enA', prm['gdn_a_log'], 8)
        p.act(nA[:], nA[:], AF.Exp)
        p.ts(nA[:], nA[:], -1.0, ALU.mult)
        dtb = p.bc('edtb', prm['gdn_dt_bias'], 8)
        xin = [p.sb('exin%d' % j, [128, 1536]) for j in range(5)]
        acc = p.sb('eacc', [128, 1536])
        acc2 = p.sb('eacc2', [128, 1536])
        tmp = p.sb('etmp', [128, 1536])
        zabw = p.sb('ezab', [128, 128])
        zab = zabw[:, 116:128]
        stat = p.sb('estat', [128, 16])
        lg = p.sb('elg', [128, 8])
        gg = p.sb('egg', [128, 8])
        beta = p.sb('ebeta', [128, 4])
        kb = p.sb('ekb', [128, 512])
        ka = p.sb('eka', [128, 1024])
        lw = p.sb('elw', [128, 1024])
        for t in range(ntok // 128):
            r0 = t * 128
            rows = slice(r0, r0 + 128)
            for j in range(3):
                p.dma(xin[j][:], projh[r0 + 1 + j:r0 + 129 + j, 0:1536], q=('sp', 'act')[j % 2])
            p.tt(acc[:], xin[0][:], hw[0][:], ALU.mult)
            for j in (1, 2):
                p.tt(tmp[:], xin[j][:], hw[j][:], ALU.mult, eng='pool')
                p.tt(acc[:], acc[:], tmp[:], ALU.add)
            p.tt(acc[:], acc[:], hbias[:], ALU.add)
            p.dma(outs['hx1'][rows, :], acc[:, 0:512], q='pool')
            p.dma(outs['hx2'][rows, :], acc[:, 512:1024], q='pool')
            p.dma(outs['hv'][rows, :], acc[:, 1024:1536], q='pool')
            for j in range(5):
                p.dma(xin[j][:], projh[r0 + j:r0 + 128 + j, 1536:3072], q=('sp', 'act')[j % 2])
            p.dma(zabw[:], projh[r0 + 2:r0 + 130, 3468:3596])
            p.tt(acc2[:], xin[0][:], gw[0][:], ALU.mult)
            for j in range(1, 5):
                p.tt(tmp[:], xin[j][:], gw[j][:], ALU.mult, eng='pool')
                p.tt(acc2[:], acc2[:], tmp[:], ALU.add)
            p.act(acc2[:], acc2[:], AF.Silu)
            q, k, v = acc2[:, 0:512], acc2[:, 512:1024], acc2[:, 1024:1536]
            p.dma(outs['gv'][rows, :], v, q='pool')
            for i, (src, scale) in enumerate(((q, 128.0 ** -0.5), (k, 1.0))):
                s8 = stat[:, 4 * i:4 * i + 4]
                p.tt(_h3(tmp[:, 0:512], 4), _h3(src, 4), _h3(src, 4), ALU.mult)
                p.red(s8, _h3(tmp[:, 0:512], 4))
                p.rsqrt(s8, s8, 1e-6)
                if scale != 1.0:
                    p.ts(s8, s8, float(scale), ALU.mult)
                p.tt(_h3(src, 4), _h3(src, 4), _bcol(s8, 4, 128), ALU.mult)
            p.dma(outs['gq'][rows, :], q, q='pool')
            p.dma(outs['gkk'][rows, :], k, q='pool')
            p.act(beta[:], zab[:, 8:12], AF.Sigmoid)
            p.tt(lg[:], zab[:, 0:8], dtb[:], ALU.add)
            p.act(lg[:], lg[:], AF.Exp)
            p.act(lg[:], lg[:], AF.Ln, bias=1.0)
            p.tt(lg[:], lg[:], nA[:], ALU.mult)
            p.act(gg[:], lg[:], AF.Exp)
            p.tt(_h3(kb[:], 4), _h3(k, 4), _bcol(beta[:], 4, 128), ALU.mult)
            p.dma(outs['gkb'][rows, :], kb[:], q='pool')
            for d in range(2):
                sl = slice(512 * d, 512 * (d + 1))
                p.tt(_h3(ka[:, sl], 4), _h3(kb[:], 4), _bcol(gg[:, 4 * d:4 * d + 4], 4, 128), ALU.mult)
                p.cp(_h3(lw[:, sl], 4), _bcol(lg[:, 4 * d:4 * d + 4], 4, 128), eng='pool')
            p.dma(outs['gka'][rows, :], ka[:], q='pool')
            p.dma(outs['glw'][rows, :], lw[:], q='pool')


def post_even(p, ins, prm, ymix, ntok):
    with p.scope():
        nw = p.sb('rnw', [128, 512])
        for h in range(4):
            p.dma(nw[:, 128 * h:128 * (h + 1)], prm['gdn_norm_w'].rearrange('(o f) -> o f', o=1).broadcast_to([128, 128]), q='act')
        names = ['of', 'ob', 'zg', 'ya']
        X = [{n: p.sb('r%s%d' % (n, i), [128, 512]) for n in names} for i in range(2)]
        tmp = p.sb('rtmp', [128, 512])
        o = p.sb('ro', [128, 512])
        stat = p.sb('rstat', [128, 16])
        yo = [p.sb('ryo%d' % i, [128, 1024]) for i in range(2)]
        for t in range(ntok // 128):
            rows = slice(t * 128, (t + 1) * 128)
            x, Y = X[t % 2], yo[t % 2]
            for i, n in enumerate(names):
                p.dma(x[n][:], ins[n][rows, :], q=('sp', 'act')[i % 2])
            p.cp(Y[:, 0:512], x['ya'][:], eng='pool')
            p.tt(o[:], x['of'][:], x['ob'][:], ALU.add)
            head_norm(p, o[:], o[:], 4, 128, 1e-6, tmp[:], stat[:])
            p.tt(o[:], o[:], nw[:], ALU.mult)
            p.act(x['zg'][:], x['zg'][:], AF.Silu)
            p.tt(Y[:, 512:1024], o[:], x['zg'][:], ALU.mult)
            p.dma(ymix[rows, :], Y[:], q='pool')


def filter_consts(L):
    pos = np.arange(L, dtype=np.float32)[:, None]
    t = pos / np.float32(max(L - 1, 1))
    bands = np.linspace(1e-4, 15, 16, dtype=np.float32)[None, :]
    ang = bands * np.float32(2.0 * np.pi / L) * pos
    z = np.concatenate([t, np.cos(ang), -np.sin(ang)], -1).astype(np.float32)
    max_decay = np.log(1e-2) / 0.3
    min_decay = np.log(1e-2) / 1.5
    deltas = np.abs(np.linspace(min_decay, max_decay, 512, dtype=np.float32))
    window = np.exp(-t * deltas).astype(np.float32)
    return z, window


def filter_program(p, ins, outs, npos):
    with p.scope():
        zT = p.sb('fzT', [33, npos])
        p.dma(zT[:], ins['zT'])
        w1 = p.sb('fw1', [33, 64])
        p.dma(w1[:], ins['f_w1'])
        w2 = p.sb('fw2', [64, 64])
        p.dma(w2[:], ins['f_w2'], q='act')
        w3 = p.sb('fw3', [64, 64])
        p.dma(w3[:], ins['f_w3'])
        w4 = p.sb('fw4', [64, 2048])
        p.dma(w4[:], ins['f_w4'], q='act')
        bf = p.sb('fbf', [64, 128])
        p.dma(bf[:], ins['f_bf'])
        hT = [p.sb('fhT%d' % i, [64, npos]) for i in range(3)]
        arg = p.sb('farg', [64, npos])
        wm1 = p.sb('fwm1', [64, npos])
        wm2 = p.sb('fwm2', [64, npos])
        win = p.sb('fwin', [128, 512])
        ho = [p.sb('fho%d' % i, [128, 512]) for i in range(2)]
        pf = [p.ps('fpf%d' % i, [128, 512]) for i in range(4)]
        srcs = [(w1, zT, 33), (w2, hT[0], 64), (w3, hT[1], 64)]
        for li, (w, src, K) in enumerate(srcs):
            for c0 in range(0, npos, 512):
                cs = slice(c0, min(npos, c0 + 512))
                P = pf[li % 2][0:64, 0:cs.stop - cs.start]
                p.mm(P, w[0:K, :], src[0:K, cs])
                p.ts(arg[:, cs], P, bf[:, li:li + 1], ALU.add, bf[:, 3:4], ALU.mult)
                p.wrap(arg[:, cs], arg[:, cs], wm1[:, cs], wm2[:, cs])
                p.act(hT[li][:, cs], arg[:, cs], AF.Sin)
        nb = 0
        for t in range(npos // 128):
            rows = slice(t * 128, (t + 1) * 128)
            p.dma(win[:], ins['window'][rows, :])
            for blk in range(4):
                P = pf[2 + nb % 2]
                Y = ho[nb % 2]
                p.mm(P[:, :], hT[2][:, rows], w4[:, 512 * blk:512 * (blk + 1)])
                p.tt(Y[:], P[:, :], win[:], ALU.mult)
                p.dma(outs['h'][rows, 512 * blk:512 * (blk + 1)], Y[:], q='pool')
                nb += 1


def fft_consts():
    c = np.zeros((128, 800), np.float64)
    a = np.arange(64)[:, None]
    k1 = np.arange(64)[None, :]
    th = 2 * np.pi * a * k1 / 64
    c[0:64, 0:64] = np.cos(th)
    c[0:64, 64:128] = -np.sin(th)
    pp = np.arange(128)[:, None]
    tw = 2 * np.pi * pp * k1 / 8192
    c[:, 128:192] = np.cos(tw)
    c[:, 192:256] = np.sin(tw)
    k2 = np.arange(128)[None, :]
    t2 = 2 * np.pi * pp * k2 / 128
    c[:, 256:384] = np.cos(t2)
    c[:, 384:512] = np.sin(t2)
    c[:, 512:640] = -np.sin(t2)
    a32 = np.arange(32)[None, :]
    k1c = np.arange(64)[:, None]
    thi = 2 * np.pi * a32 * k1c / 64
    c[0:64, 640:672] = np.cos(thi) / 8192
    c[64:128, 640:672] = -np.sin(thi) / 8192
    c[:, 672:800] = np.eye(128)
    return c.astype(np.float32)


RC = 16


class HyenaConv:
    def __init__(self, p, cdram):
        self.p = p
        c = p.sb('hc', [128, 800])
        p.dma(c[:], cdram)
        self.F1, self.Tc, self.Ts = c[0:64, 0:128], c[:, 128:192], c[:, 192:256]
        self.C2, self.S2, self.nS2 = c[:, 256:384], c[:, 384:512], c[:, 512:640]
        self.F1i, self.ident = c[:, 640:672], c[:, 672:800]
        sb, ps = p.sb, p.ps
        self.yre, self.yim = sb('hyre', [128, RC, 64]), sb('hyim', [128, RC, 64])
        self.t1, self.t2 = sb('ht1', [128, RC, 64]), sb('ht2', [128, RC, 64])
        self.r2 = sb('hr2', [128, RC, 128])
        self.rT = sb('hrT', [128, RC, 128])
        self.pA = [ps('hpA%d' % i, [128, 512]) for i in range(2)]
        self.pB = [ps('hpB%d' % i, [128, 512]) for i in range(4)]
        self.pC = [ps('hpC%d' % i, [128, 512]) for i in range(2)]

    def _tw(self, ore, oim, re, im, sign):
        p = self.p
        Tc = self.Tc.unsqueeze(1).broadcast_to([128, RC, 64])
        Ts = self.Ts.unsqueeze(1).broadcast_to([128, RC, 64])
        t1, t2 = self.t1[:], self.t2[:]
        p.tt(t1, re, Tc, ALU.mult)
        p.tt(t2, im, Ts, ALU.mult, eng='pool')
        p.tt(ore, t1, t2, ALU.add if sign > 0 else ALU.subtract)
        p.tt(t1, im, Tc, ALU.mult)
        p.tt(t2, re, Ts, ALU.mult, eng='pool')
        p.tt(oim, t1, t2, ALU.subtract if sign > 0 else ALU.add)

    def fwd(self, X, K, dre, dim):
        p = self.p
        for r in range(RC):
            P = self.pA[(r // 4) % 2]
            p.mm(P[:, (r % 4) * 128:(r % 4 + 1) * 128], X[0:K, r, :], self.F1[0:K, :])
            if r % 4 == 3:
                P3 = P[:, :].rearrange('p (r c) -> p r c', r=4)
                p.cp(self.yre[:, r - 3:r + 1, :], P3[:, :, 0:64], eng='act')
                p.cp(self.yim[:, r - 3:r + 1, :], P3[:, :, 64:128], eng='dve')
        self._tw(self.r2[:, :, 0:64], self.r2[:, :, 64:128], self.yre[:], self.yim[:], +1)
        for cb in range(RC // 8):
            rs = slice(8 * cb, 8 * cb + 8)
            re, im = self.r2[:, rs, 0:64], self.r2[:, rs, 64:128]
            Pre, Pim = self.pB[(2 * cb) % 4], self.pB[(2 * cb + 1) % 4]
            p.mm(Pre[:, :], self.C2, re, start=True, stop=False)
            p.mm(Pre[:, :], self.S2, im, start=False, stop=True)
            p.mm(Pim[:, :], self.C2, im, start=True, stop=False)
            p.mm(Pim[:, :], self.nS2, re, start=False, stop=True)
            p.cp(dre[:, rs, :], Pre[:, :].rearrange('p (r c) -> p r c', r=8), eng='act')
            p.cp(dim[:, rs, :], Pim[:, :].rearrange('p (r c) -> p r c', r=8), eng='dve')

    def mul(self, ore, oim, are, aim, bre, bim):
        p = self.p
        t1, t2 = self.t1[:], self.t2[:]
        p.tt(t1, are, bre, ALU.mult)
        p.tt(t2, aim, bim, ALU.mult, eng='pool')
        p.tt(ore, t1, t2, ALU.subtract)
        p.tt(t1, are, bim, ALU.mult)
        p.tt(t2, aim, bre, ALU.mult, eng='pool')
        p.tt(oim, t1, t2, ALU.add)

    def inv(self, pre, pim, epilogue):
        p = self.p
        for cb in range(RC // 8):
            rs = slice(8 * cb, 8 * cb + 8)
            re, im = pre[:, rs, :], pim[:, rs, :]
            Pre, Pim = self.pB[(2 * cb) % 4], self.pB[(2 * cb + 1) % 4]
            p.mm(Pre[:, :], self.C2, re, start=True, stop=False)
            p.mm(Pre[:, :], self.nS2, im, start=False, stop=True)
            p.mm(Pim[:, :], self.S2, re, start=True, stop=False)
            p.mm(Pim[:, :], self.C2, im, start=False, stop=True)
            p.cp(self.yre[:, rs, :], Pre[:, :].rearrange('p (r c) -> p r c', r=8), eng='act')
            p.cp(self.yim[:, rs, :], Pim[:, :].rearrange('p (r c) -> p r c', r=8), eng='dve')
        self._tw(self.r2[:, :, 0:64], self.r2[:, :, 64:128], self.yre[:], self.yim[:], -1)
        for r in range(RC):
            P = self.pA[(r // 4) % 2]
            dst = P[:, (r % 4) * 128:(r % 4 + 1) * 128]
            src = self.r2[:, r, :]
            p.op('pe', lambda e, dst=dst, src=src: e.transpose(dst, src, self.ident), [src, self.ident], [dst])
            if r % 4 == 3:
                p.cp(self.rT[:, r - 3:r + 1, :], P[:, :].rearrange('p (r c) -> p r c', r=4), eng=('act', 'dve')[(r // 4) % 2])
        for g in range(RC // 4):
            PS = self.pC[g % 2]
            p.mm(PS[0:32, :], self.F1i, self.rT[:, 4 * g:4 * g + 4, :])
            epilogue(g, PS[0:32, :].rearrange('p (r c) -> p r c', r=4))


def hyena_program(p, ins, outs, R):
    with p.scope():
        H = HyenaConv(p, ins['fconst'])
        sb = p.sb
        XA, XB = sb('hXA', [64, RC, 128]), sb('hXB', [64, RC, 128])
        V, X1, X2, Z, O = (sb('h' + n, [32, RC, 128]) for n in ('V', 'X1', 'X2', 'Z', 'O'))
        tt_ = sb('htt', [32, 4, 128])
        sph = [(sb('hsr%d' % o, [128, RC, 64]), sb('hsi%d' % o, [128, RC, 64])) for o in range(2)]
        ure, uim = sb('hure', [128, RC, 64]), sb('huim', [128, RC, 64])
        pre, pim = sb('hpre', [128, RC, 64]), sb('hpim', [128, RC, 64])
        skp = [sb('hskp%d' % o, [32, R]) for o in range(2)]
        for o in range(2):
            p.dma(skp[o][:], ins['skip'][o:o + 1, :].broadcast_to([32, R]), q='act')
        for ch in range(R // RC):
            rows = slice(ch * RC, (ch + 1) * RC)
            for o in range(2):
                p.dma(XA[:], ins['ha'][o, rows, :].rearrange('r (a q) -> a r q', a=64))
                p.dma(XB[:], ins['hb'][o, rows, :].rearrange('r (a q) -> a r q', a=64), q='act')
                p.tt(XA[:], XA[:], XB[:], ALU.add)
                H.fwd(XA, 64, sph[o][0], sph[o][1])
            for T_, nm in ((V, 'v'), (X1, 'x1'), (X2, 'x2')):
                p.dma(T_[:], ins[nm][rows, :].rearrange('r (a q) -> a r q', a=32), q='act')

            def mk_epi(gate, base, skt, dst):
                def epi(g, PS):
                    rs = slice(4 * g, 4 * g + 4)
                    sk = skt[:, ch * RC + 4 * g:ch * RC + 4 * g + 4].unsqueeze(2).broadcast_to([32, 4, 128])
                    p.tt(tt_[:], base[:, rs, :], sk, ALU.mult, eng='pool')
                    p.tt(tt_[:], tt_[:], PS, ALU.add)
                    p.tt(dst[:, rs, :], tt_[:], gate[:, rs, :], ALU.mult)
                return epi
            H.fwd(V, 32, ure, uim)
            H.mul(pre[:], pim[:], ure[:], uim[:], sph[0][0][:], sph[0][1][:])
            H.inv(pre, pim, mk_epi(X1, V, skp[0], Z))
            H.fwd(Z, 32, ure, uim)
            H.mul(pre[:], pim[:], ure[:], uim[:], sph[1][0][:], sph[1][1][:])
            H.inv(pre, pim, mk_epi(X2, Z, skp[1], O))
            p.dma(outs['y'][rows, :].rearrange('r (a q) -> a r q', a=32), O[:], q='pool')


def layer_even(x2, W, B, T, n):
    ntok = B * T // n
    ident = np.eye(128, dtype=np.float32)
    xh = shard_halo(x2, B, T, n, halo=2)
    pk = ['hy_conv_w', 'hy_conv_b', 'gdn_conv_w', 'gdn_a_log', 'gdn_dt_bias']
    in_maps = [dict({'xh': xh[c], 'w_in': f32c(W['w_in']), 'ident': ident}, **{k: f32c(W[k]) for k in pk}) for c in range(n)]
    onames = {'hx1': 512, 'hx2': 512, 'hv': 512, 'gv': 512, 'gq': 512, 'gkk': 512, 'gkb': 512, 'gka': 1024, 'glw': 1024, 'zg': 512}

    def body1(p, ins, outs, scr):
        idn = load_ident(p, ins)
        linear_program(p, idn, ins['xh'], ins['w_in'], scr['projh'], ntok + 128, D, 3596, tag='li')
        p.dma(outs['zg'], scr['projh'][2:ntok + 2, 3072:3584])
        prep_even(p, scr['projh'], ins, outs, ntok)
    r1 = launch(body1, in_maps, {k: (ntok, v) for k, v in onames.items()}, {'projh': (ntok + 128, 3596)})
    full = {k: np.concatenate([r1[c][k] for c in range(n)], 0).reshape(B, T, -1) for k in onames}

    z, window = filter_consts(T)
    npos = T // n
    bf = np.zeros((64, 128), np.float32)
    bf[:, 0], bf[:, 1], bf[:, 2], bf[:, 3] = W['f_b1'], W['f_b2'], W['f_b3'], W['f_freq']
    in_maps = [dict(zT=f32c(z[c * npos:(c + 1) * npos].T), window=f32c(window[c * npos:(c + 1) * npos]), f_bf=bf,
                    f_w1=f32c(W['f_w1']), f_w2=f32c(W['f_w2']), f_w3=f32c(W['f_w3']), f_w4=f32c(W['f_w4'])) for c in range(n)]
    r2 = launch(lambda p, ins, outs, scr: filter_program(p, ins, outs, npos), in_maps, {'h': (npos, 2048)})
    hf = np.concatenate([r2[c]['h'] for c in range(n)], 0).reshape(T, 2, 2, 512)

    Rr = B * 512 // n
    rowsT = {k: f32c(np.transpose(full[k], (0, 2, 1)).reshape(B * 512, T)) for k in ('hv', 'hx1', 'hx2')}
    ha = np.zeros((2, 512, 2 * T), np.float32)
    hb = np.zeros((2, 512, 2 * T), np.float32)
    for o in range(2):
        ha[o, :, 0:T] = hf[:, o, 0, :].T
        ha[o, :, T + 1:] = hf[:0:-1, o, 1, :].T
        hb[o, :, 0] = hf[0, o, 1, :]
    fc = fft_consts()
    in_maps = []
    for c in range(n):
        rs = slice(c * Rr, (c + 1) * Rr)
        ch = np.arange(c * Rr, (c + 1) * Rr) % 512
        in_maps.append(dict(v=rowsT['hv'][rs], x1=rowsT['hx1'][rs], x2=rowsT['hx2'][rs], ha=f32c(ha[:, ch]), hb=f32c(hb[:, ch]),
                            skip=f32c(W['hy_skip'][:, ch]), fconst=fc))
    r3 = launch(lambda p, ins, outs, scr: hyena_program(p, ins, outs, Rr), in_maps, {'y': (Rr, T)})
    ya = np.concatenate([r3[c]['y'] for c in range(n)], 0).reshape(B, 512, T).transpose(0, 2, 1)

    ub = []
    for b in range(B):
        for zz in range(2):
            for h in range(4):
                hs = slice(128 * h, 128 * (h + 1))
                zs = slice(512 * zz + 128 * h, 512 * zz + 128 * (h + 1))
                arrs = [full['gq'][b, :, hs], full['gkb'][b, :, hs], full['gv'][b, :, hs], full['glw'][b, :, zs],
                        full['gkk'][b, :, hs], full['gka'][b, :, zs]]
                ub.append(np.stack([a[::-1] if zz else a for a in arrs]))
    _, yb = run_recur([], None, ub, (128, 128, True), T, n)
    o = np.zeros((2, B, T, 512), np.float32)
    i = 0
    for b in range(B):
        for zz in range(2):
            for h in range(4):
                o[zz, b, :, 128 * h:128 * (h + 1)] = yb[i][::-1] if zz else yb[i]
                i += 1

    def sh(a):
        return np.split(f32c(a.reshape(B * T, -1)), n, 0)
    parts = {'x': sh(x2), 'of': sh(o[0]), 'ob': sh(o[1]), 'zg': sh(full['zg']), 'ya': sh(ya)}
    wk = ['w_out', 'ln_g', 'ln_b', 'mlp_w1', 'mlp_w2', 'gdn_norm_w']
    in_maps = [dict({k: f32c(v[c]) for k, v in parts.items()}, ident=ident, **{k: f32c(W[k]) for k in wk}) for c in range(n)]

    def body5(p, ins, outs, scr):
        idn = load_ident(p, ins)
        post_even(p, ins, ins, scr['ymix'], ntok)
        dense_tail(p, idn, ins['x'], scr['ymix'], ins, outs['xo'], scr, ntok, 0)
    r5 = launch(body5, in_maps, {'xo': (ntok, D)},
                {'ymix': (ntok, D), 'mix': (ntok, D), 'x1': (ntok, D), 'hmid': (ntok, 4 * D), 'mlp': (ntok, D)})
    return np.concatenate([r5[c]['xo'] for c in range(n)], 0)


def kernel(**inp):
    B, T, n = 4, 4096, 8
    g = lambda k: np.asarray(inp[k], dtype=np.float32)
    x2 = f32c(g('x')).reshape(B * T, D)
    W0 = dict(w_in=g('ev_w_in')[0], hy_conv_w=g('hy_conv_w')[0], hy_conv_b=g('hy_conv_b')[0], gdn_conv_w=g('gdn_conv_w')[0],
              gdn_a_log=g('gdn_a_log')[0].reshape(-1), gdn_dt_bias=g('gdn_dt_bias')[0].reshape(-1), gdn_norm_w=g('gdn_norm_w')[0],
              f_w1=g('hy_filt_w1')[0], f_b1=g('hy_filt_b1')[0], f_w2=g('hy_filt_w2')[0], f_b2=g('hy_filt_b2')[0],
              f_w3=g('hy_filt_w3')[0], f_b3=g('hy_filt_b3')[0], f_w4=g('hy_filt_w4')[0], f_freq=g('hy_filt_freq')[0],
              hy_skip=g('hy_skip')[0], w_out=g('ev_w_out')[0], ln_g=g('ln_g')[0], ln_b=g('ln_b')[0],
              mlp_w1=g('mlp_w1')[0], mlp_w2=g('mlp_w2')[0])
    W1 = dict(w_in=g('od_w_in')[0], hg_lower=g('hg_lower'), hg_norm_w=g('hg_norm_w')[0], rw_mu=g('rw_mu')[0], rw_w0=g('rw_w0')[0],
              rw_w2=g('rw_w2')[0], rw_a0=g('rw_a0')[0], rw_a2=g('rw_a2')[0], rw_g2=g('rw_g2')[0], rw_k_k=g('rw_k_k')[0],
              rw_k_a=g('rw_k_a')[0], rw_r_k=g('rw_r_k')[0].reshape(-1), rw_lnx_w=g('rw_lnx_w')[0], rw_lnx_b=g('rw_lnx_b')[0],
              w_out=g('od_w_out')[0], ln_g=g('ln_g')[1], ln_b=g('ln_b')[1], mlp_w1=g('mlp_w1')[1], mlp_w2=g('mlp_w2')[1])
    x2 = layer_even(x2, W0, B, T, n)
    x2 = layer_odd(x2, W1, B, T, n)
    return f32c(x2.reshape(B, T, D))
```

```python
import contextlib
import itertools
import numpy as np
import concourse.bass as bass
import concourse.mybir as mybir
from concourse.bass_utils import run_bass_kernel_spmd

F32 = mybir.dt.float32
BF16 = mybir.dt.bfloat16
AF = mybir.ActivationFunctionType
ALU = mybir.AluOpType
AX = mybir.AxisListType

N_DMA_SEMS = 6


def _box(ap):
    name = ap.tensor.name
    dims = list(ap.ap)
    sp = str(ap.space).lower()
    if 'dram' in sp or 'hbm' in sp:
        lo = hi = int(ap.offset)
        for s, c in dims:
            d = (int(c) - 1) * int(s)
            if d < 0:
                lo += d
            else:
                hi += d
        return (name, 0, 1, lo, hi + 1)
    if 'psum' in sp:
        return (name, 0, 128, 0, 1 << 30)
    p0 = ap.start_partition
    p0 = int(p0() if callable(p0) else p0)
    pn = ap.partition_size
    pn = int(pn() if callable(pn) else pn)
    pstep = int(dims[0][0])
    off = int(ap.offset) % pstep if pstep > 0 else int(ap.offset)
    lo = hi = off
    for s, c in dims[1:]:
        d = (int(c) - 1) * int(s)
        if d < 0:
            lo += d
        else:
            hi += d
    return (name, p0, p0 + pn, lo, hi + 1)


def _ov(a, b):
    return a[1] < b[2] and b[1] < a[2] and a[3] < b[4] and b[3] < a[4]


def _contains(a, b):
    return a[1] <= b[1] and b[2] <= a[2] and a[3] <= b[3] and b[4] <= a[4]


class Prog:
    ENG = ('pe', 'act', 'dve', 'pool', 'sp')

    def __init__(self):
        self.nc = bass.Bass('TRN2', target_bir_lowering=False)
        nc = self.nc
        self.eng = {'pe': nc.tensor, 'act': nc.scalar, 'dve': nc.vector, 'pool': nc.gpsimd, 'sp': nc.sync}
        self.stack = contextlib.ExitStack()
        self.sem = {}
        self.cnt = {}
        self.known = {e: {} for e in self.ENG}
        self.track = {}
        self.dma_rr = {e: 0 for e in self.ENG}
        self.nops = 0
        self.uid = 0
        for e in ('pe', 'act', 'dve', 'pool'):
            self._mksem('c_' + e)
        for e in ('sp', 'act', 'pool'):
            for j in range(N_DMA_SEMS):
                self._mksem('d_%s%d' % (e, j))

    def _mksem(self, name):
        self.sem[name] = self.stack.enter_context(self.nc.semaphore(name))
        self.cnt[name] = 0

    def dram(self, name, shape, dt, kind):
        return self.nc.dram_tensor(name, list(shape), dt, kind=kind).ap()

    def sb(self, name, shape, dt=F32):
        return self.stack.enter_context(self.nc.sbuf_tensor(name, list(shape), dt))

    def ps(self, name, shape, dt=F32):
        return self.stack.enter_context(self.nc.psum_tensor(name, list(shape), dt))

    def _deps(self, reads, writes, self_sem):
        need = {}

        def req(sv):
            if sv is None:
                return
            s, v = sv
            if need.get(s, 0) < v:
                need[s] = v
        for ap in reads:
            b = _box(ap)
            for rec in self.track.get(b[0], ()):
                if _ov(rec[0], b):
                    req(rec[1])
        for ap in writes:
            b = _box(ap)
            for rec in self.track.get(b[0], ()):
                if _ov(rec[0], b):
                    req(rec[1])
                    for s, v in rec[2].items():
                        req((s, v))
        return need

    def _commit(self, reads, writes, sem, val):
        for ap in reads:
            b = _box(ap)
            lst = self.track.setdefault(b[0], [])
            exact = False
            for rec in lst:
                if _ov(rec[0], b):
                    rec[2][sem] = val
                    if rec[0] == b or _contains(rec[0], b):
                        exact = True
            if not exact:
                lst.append([b, None, {sem: val}])
        for ap in writes:
            b = _box(ap)
            lst = self.track.setdefault(b[0], [])
            lst[:] = [r for r in lst if not _contains(b, r[0])]
            lst.append([b, (sem, val), {}])

    def _emit(self, eng, sem, inc, fn, reads, writes, skip_self=False):
        psr = [a for a in reads if 'psum' in str(a.space).lower()]
        if psr:
            writes = list(writes) + psr
        need = self._deps(reads, writes, sem)
        e = self.eng[eng]
        kn = self.known[eng]
        for s, v in need.items():
            if skip_self and s == sem:
                continue
            if kn.get(s, 0) < v:
                e.wait_ge(self.sem[s], v)
                kn[s] = v
        inst = fn(e)
        self.cnt[sem] += inc
        inst.then_inc(self.sem[sem], inc)
        self._commit(reads, writes, sem, self.cnt[sem])
        self.nops += 1
        return inst

    def op(self, eng, fn, reads, writes):
        return self._emit(eng, 'c_' + eng, 1, fn, reads, writes, skip_self=(eng == 'pe'))

    def dma(self, out, in_, q='sp', **kw):
        j = self.dma_rr[q]
        self.dma_rr[q] = (j + 1) % N_DMA_SEMS
        sem = 'd_%s%d' % (q, j)
        return self._emit(q, sem, 16, lambda e: e.dma_start(out=out, in_=in_, **kw), [in_], [out])

    def finish(self, eng='sp'):
        e = self.eng[eng]
        for s, v in self.cnt.items():
            if v > 0:
                e.wait_ge(self.sem[s], v)

    def mm(self, out, lhsT, rhs, start=True, stop=True):
        return self.op('pe', lambda e: e.matmul(out, lhsT, rhs, start=start, stop=stop), [lhsT, rhs], [out])

    def tr(self, out, in_, ident):
        return self.op('pe', lambda e: e.matmul(out, in_, ident, start=True, stop=True), [in_, ident], [out])

    def act(self, out, in_, func, bias=None, scale=None, accum=None, eng='act'):
        kw = {}
        rd = [in_]
        wr = [out]
        if bias is not None:
            kw['bias'] = bias
            if not isinstance(bias, (int, float)):
                rd.append(bias)
        if scale is not None:
            kw['scale'] = scale
            if not isinstance(scale, (int, float)):
                rd.append(scale)
        if accum is not None:
            kw['accum_out'] = accum
            wr.append(accum)
        return self.op('act', lambda e: e.activation(out, in_, func, **kw), rd, wr)

    def tt(self, out, a, b, op, eng='dve'):
        return self.op(eng, lambda e: e.tensor_tensor(out, a, b, op), [a, b], [out])

    def ts(self, out, a, s1, op0, s2=None, op1=None, eng='dve', accum=None):
        rd = [a]
        for s in (s1, s2):
            if s is not None and not isinstance(s, (int, float)):
                rd.append(s)
        wr = [out] + ([accum] if accum is not None else [])
        kw = {}
        if op1 is not None:
            kw['op1'] = op1
        if accum is not None:
            kw['accum_out'] = accum
        return self.op(eng, lambda e: e.tensor_scalar(out, a, s1, s2, op0, **kw), rd, wr)

    def stt(self, out, a, s, b, op0, op1, accum=None):
        rd = [a, b] + ([] if isinstance(s, (int, float)) else [s])
        wr = [out] + ([accum] if accum is not None else [])
        kw = {'accum_out': accum} if accum is not None else {}
        return self.op('dve', lambda e: e.scalar_tensor_tensor(out, a, s, b, op0, op1, **kw), rd, wr)

    def cp(self, out, in_, eng='dve'):
        if eng == 'act':
            return self.op('act', lambda e: e.copy(out, in_), [in_], [out])
        return self.op(eng, lambda e: e.tensor_copy(out, in_), [in_], [out])

    def memset(self, ap, val, eng='dve'):
        return self.op(eng, lambda e: e.memset(ap, val), [], [ap])

    def close(self):
        self.stack.close()

    @contextlib.contextmanager
    def scope(self):
        outer = self.stack
        self.stack = contextlib.ExitStack()
        try:
            yield
        finally:
            self.barrier()
            self.stack.close()
            self.stack = outer

    def barrier(self):
        for en in ('pe', 'act', 'dve', 'pool', 'sp'):
            e = self.eng[en]
            kn = self.known[en]
            for sname, v in self.cnt.items():
                if v > 0 and kn.get(sname, 0) < v:
                    e.wait_ge(self.sem[sname], v)
                    kn[sname] = v

    def red(self, out, in_, op=None, eng='dve'):
        op = op or ALU.add
        return self.op(eng, lambda e: e.tensor_reduce(out, in_, AX.X, op), [in_], [out])

    def recip(self, out, in_):
        return self.op('dve', lambda e: e.reciprocal(out, in_), [in_], [out])

    def rsqrt(self, out, in_, eps):
        self.ts(out, in_, float(eps), ALU.add)
        self.act(out, out, AF.Sqrt)
        self.recip(out, out)

    def wrap(self, out, in_, m1, m2):
        self.ts(m1, in_, -float(np.pi), ALU.is_lt)
        self.ts(m2, in_, float(np.pi), ALU.is_gt)
        self.stt(out, m1, float(2 * np.pi), in_, ALU.mult, ALU.add)
        self.stt(out, m2, -float(2 * np.pi), out, ALU.mult, ALU.add)

    def bc(self, name, vec, n):
        t = self.sb(name, [128, n])
        self.dma(t[:], vec.rearrange('(o f) -> o f', o=1).broadcast_to([128, n]), q='act')
        return t


CH = 32
NCH = 128 // CH


def recur_consts():
    t = np.arange(128)
    same = (t[:, None] // CH) == (t[None, :] // CH)
    ident = np.eye(128, dtype=np.float32)
    tri = (same & (t[:, None] <= t[None, :])).astype(np.float32)
    blk = same.astype(np.float32)
    msl = (same & (t[None, :] < t[:, None])).astype(np.float32)
    msu = (same & (t[:, None] < t[None, :])).astype(np.float32)
    return np.concatenate([ident, tri, blk, msl, msu, msl, tri, -tri], axis=1).astype(np.float32)


class Recur:
    def __init__(self, p, cdram, dk, dv, delta, tag=''):
        self.p, self.dk, self.dv, self.delta = p, dk, dv, delta
        g = tag
        self.c = p.sb(g + 'rc', [128, 8 * 128])
        p.dma(self.c[:], cdram)
        c = self.c
        self.ident, self.tri, self.blk = c[:, 0:128], c[:, 128:256], c[:, 256:384]
        self.msl, self.msu = c[:, 384:512], c[:, 512:640]
        self.mcat = c[:, 384:896]
        self.ntri = c[:, 896:1024]
        sb, ps = p.sb, p.ps
        nin = 6 if delta else 4
        self.inp = [sb(g + 'in%d' % i, [128, nin, 128]) for i in range(2)]
        self.gs = sb(g + 'gs', [128, 6, 128])
        self.ex = sb(g + 'ex', [128, 5, 128])
        self.sc = sb(g + 'sc', [128, 4, 128])
        self.kd = sb(g + 'kd', [128, 128])
        self.kad = sb(g + 'kad', [128, 128])
        self.vb = sb(g + 'vb', [128, 128])
        self.fT = sb(g + 'fT', [128, 4, 128])
        self.elc = sb(g + 'elc', [128, NCH])
        self.sS = sb(g + 'sS', [128, 3, 128])
        self.arkT = sb(g + 'arkT', [128, 128])
        self.araT = sb(g + 'araT', [128, 128])
        self.nm = sb(g + 'nm', [128, 24, 128])
        self.wmT = sb(g + 'wmT', [128, 128])
        self.tbT = sb(g + 'tbT', [128, 128])
        self.u0 = sb(g + 'u0', [128, 128])
        self.ub = sb(g + 'ub', [128, 128])
        self.ys = sb(g + 'ys', [128, 128])
        self.yo = [sb(g + 'yo%d' % i, [128, 128]) for i in range(2)]
        self.M = sb(g + 'M', [128, 128])
        self.Mb = sb(g + 'Mb', [128, 128])
        self.pb = [ps(g + 'pb%d' % i, [128, 512]) for i in range(4)]

    def reset(self):
        p = self.p
        if not getattr(self, '_padded', False):
            p.memset(self.fT[:].rearrange('p a b -> p (a b)'), 0.0)
            self._padded = True
        p.memset(self.M[:], 0.0)
        p.memset(self.Mb[:], 0.0, eng='pool')

    def tile(self, it, srcs, ydst):
        for _ in self.tile_gen(it, srcs, ydst):
            pass

    def tile_gen(self, it, srcs, ydst):
        p, dk, dv, delta = self.p, self.dk, self.dv, self.delta
        A, B, C, Dk = self.pb
        X = self.inp[it % 2]
        qs = ['sp', 'act']
        for i, s in enumerate(srcs):
            d = dv if i == 2 else dk
            p.dma(X[:, i, 0:d], s, q=qs[i % 2])
        yield
        r, k, v, lw = X[:, 0, 0:dk], X[:, 1, 0:dk], X[:, 2, 0:dv], X[:, 3, 0:dk]
        gs, ex, sc = self.gs, self.ex, self.sc
        p.mm(A[:, 0:dk], self.tri, lw)
        p.mm(A[:, 128:128 + dk], self.blk, lw)
        gam, gl = A[:, 0:dk], A[:, 128:128 + dk]
        G, NG, GX, GD = gs[:, 0, 0:dk], gs[:, 1, 0:dk], gs[:, 2, 0:dk], gs[:, 3, 0:dk]
        E, Ei, Ex, Ed, EL = (ex[:, i, 0:dk] for i in range(5))
        p.cp(G, gam, eng='act')
        p.act(EL, gl, AF.Exp)
        p.tt(GD, gl, G, ALU.subtract)
        yield
        p.act(E, G, AF.Exp)
        p.ts(NG, G, -80.0, ALU.max)
        p.act(Ei, NG, AF.Exp, scale=-1.0)
        p.act(Ed, GD, AF.Exp)
        rt, kh, kx, kah = (sc[:, i, 0:dk] for i in range(4))
        p.tt(rt, r, E, ALU.mult)
        p.tt(kh, k, Ei, ALU.mult, eng='pool')
        p.tt(self.kd[:, 0:dk], k, Ed, ALU.mult)
        p.cp(self.vb[:, 0:dv], v, eng='pool')
        nT = 2
        if delta:
            kk, ka = X[:, 4, 0:dk], X[:, 5, 0:dk]
            p.tt(GX, G, lw, ALU.subtract, eng='pool')
            p.act(Ex, GX, AF.Exp)
            p.tt(kx, kk, Ex, ALU.mult)
            p.tt(kah, ka, Ei, ALU.mult, eng='pool')
            p.stt(self.kad[:, 0:dk], ka, -1.0, Ed, ALU.mult, ALU.mult)
            nT = 4
        yield
        for i in range(nT):
            p.tr(B[0:dk, i * 128:(i + 1) * 128], sc[:, i, 0:dk], self.ident)
        p.tr(A[0:dk, 256:384], EL, self.ident)
        p.cp(self.fT[0:dk, 0:nT, :], B[0:dk, 0:nT * 128].rearrange('p (a b) -> p a b', a=nT), eng='act')
        p.cp(self.elc[0:dk, :], A[0:dk, 256:384].rearrange('p (a b) -> p a b', a=NCH)[:, :, 0], eng='dve')
        rT, khT, kxT, kahT = (self.fT[:, i, :] for i in range(4))
        yield
        if delta:
            p.mm(C[:, 0:128], kxT, kahT)
            p.mm(C[:, 128:256], kahT, kxT)
            p.mm(C[:, 256:384], kxT, khT)
            p.mm(C[:, 384:512], khT, rT)
            p.mm(Dk[:, 0:128], kahT, rT)
            p.tt(self.sS[:].rearrange('p a b -> p (a b)'), C[:, 0:384], self.mcat[:, 0:384], ALU.mult)
            p.tt(self.arkT[:], C[:, 384:512], self.tri, ALU.mult)
            p.tt(self.araT[:], Dk[:, 0:128], self.ntri, ALU.mult)
            yield
            TT = None
            for TT in self._solve_gen():
                yield
            p.mm(Dk[0:dk, 128:256], kx, TT)
            p.mm(Dk[:, 256:384], self.sS[:, 2, :], TT)
            p.cp(self.wmT[0:dk, :], Dk[0:dk, 128:256], eng='act')
            p.cp(self.tbT[:], Dk[:, 256:384], eng='act')
            p.mm(Dk[:, 384:384 + dv], self.tbT[:], self.vb[:, 0:dv])
            p.cp(self.u0[:, 0:dv], Dk[:, 384:384 + dv], eng='act')
        else:
            p.mm(C[:, 384:512], khT, rT)
            p.tt(self.arkT[:], C[:, 384:512], self.tri, ALU.mult)
        yield
        M, Mb = self.M, self.Mb
        pb = self.pb
        for n in range(NCH):
            sl = slice(CH * n, CH * (n + 1))
            tp0 = (0, CH * n)
            tpk = (CH * n, 0)
            p.op('pe', lambda e, sl=sl, tp0=tp0: e.matmul(C[sl, 0:dv], rT[0:dk, sl], Mb[0:dk, 0:dv], start=True, stop=True, tile_position=tp0),
                 [rT[0:dk, sl], Mb[0:dk, 0:dv]], [C[sl, 0:dv]])
            if delta:
                p.op('pe', lambda e, sl=sl, tp0=tp0: e.matmul(B[sl, 0:dv], self.wmT[0:dk, sl], Mb[0:dk, 0:dv], start=True, stop=True, tile_position=tp0),
                     [self.wmT[0:dk, sl], Mb[0:dk, 0:dv]], [B[sl, 0:dv]])
            p.cp(self.ys[sl, 0:dv], C[sl, 0:dv], eng='act')
            if delta:
                p.tt(self.ub[sl, 0:dv], self.u0[sl, 0:dv], B[sl, 0:dv], ALU.add)
            p.op('pe', lambda e, sl=sl, tpk=tpk: e.matmul(A[0:dk, 0:dv], self.kd[sl, 0:dk], self.vb[sl, 0:dv], start=True, stop=not delta, tile_position=tpk),
                 [self.kd[sl, 0:dk], self.vb[sl, 0:dv]], [A[0:dk, 0:dv]])
            if delta:
                p.op('pe', lambda e, sl=sl, tpk=tpk: e.matmul(A[0:dk, 0:dv], self.kad[sl, 0:dk], self.ub[sl, 0:dv], start=False, stop=True, tile_position=tpk),
                     [self.kad[sl, 0:dk], self.ub[sl, 0:dv]], [A[0:dk, 0:dv]])
            p.stt(M[0:dk, 0:dv], M[0:dk, 0:dv], self.elc[0:dk, n:n + 1], A[0:dk, 0:dv], ALU.mult, ALU.add)
            p.cp(Mb[0:dk, 0:dv], M[0:dk, 0:dv], eng='act')
            yield
        p.mm(Dk[:, 0:dv], self.arkT[:], self.vb[:, 0:dv], start=True, stop=not delta)
        if delta:
            p.mm(Dk[:, 0:dv], self.araT[:], self.ub[:, 0:dv], start=False, stop=True)
        yo = self.yo[it % 2]
        p.tt(yo[:, 0:dv], self.ys[:, 0:dv], Dk[:, 0:dv], ALU.add)
        p.dma(ydst, yo[:, 0:dv], q='pool')
        yield

    def _solve_gen(self):
        p = self.p
        nm, I = self.nm, self.ident
        PA, PB = self.pb[0], self.pb[1]
        A, AT = self.sS[:, 0, :], self.sS[:, 1, :]
        F = [(nm[:, 0, :], nm[:, 1, :])]
        p.tt(F[0][0], I, A, ALU.subtract)
        p.tt(F[0][1], I, AT, ALU.subtract, eng='pool')
        X, XT = A, AT
        slot = 2
        for lev in range(4):
            P0, P1 = PA[:, 0:128], PA[:, 128:256]
            p.mm(P0, XT, X)
            p.mm(P1, X, XT)
            f, fT = nm[:, slot, :], nm[:, slot + 1, :]
            if lev < 3:
                X, XT = nm[:, slot + 2, :], nm[:, slot + 3, :]
                p.cp(nm[:, slot + 2:slot + 4, :], PA[:, 0:256].rearrange('p (a b) -> p a b', a=2), eng='act')
                p.tt(f, X, I, ALU.add)
                p.tt(fT, XT, I, ALU.add, eng='pool')
            else:
                p.tt(f, P0, I, ALU.add)
                p.tt(fT, P1, I, ALU.add)
            F.append((f, fT))
            slot += 4
            yield None
        s = 18
        p.mm(PB[:, 0:128], F[0][0], F[1][1])
        p.mm(PB[:, 128:256], F[0][1], F[1][0])
        p.mm(PB[:, 256:384], F[2][0], F[3][1])
        p.mm(PB[:, 384:512], F[2][1], F[3][0])
        p.cp(nm[:, s:s + 4, :], PB[:, :].rearrange('p (a b) -> p a b', a=4), eng='act')
        G1, H1, G2, H2 = (nm[:, s + i, :] for i in range(4))
        yield None
        p.mm(PA[:, 0:128], H1, G2)
        p.mm(PA[:, 128:256], G1, H2)
        p.cp(nm[:, s + 4:s + 6, :], PA[:, 0:256].rearrange('p (a b) -> p a b', a=2), eng='act')
        H3 = nm[:, s + 5, :]
        yield None
        p.mm(PB[:, 0:128], H3, F[4][1])
        p.cp(nm[:, 17, :], PB[:, 0:128], eng='act')
        yield nm[:, 17, :]


LN_EPS = 1e-5
D = 1024
ALPHA = 4.0 ** 0.25


def _h3(ap, H):
    return ap.rearrange('p (h d) -> p h d', h=H)


def _bcol(col, H, d):
    return col.unsqueeze(2).broadcast_to([128, H, d])


def linear_program(p, ident, x, w, y, ntok, K, N, relu2=False, tag='l', f32=False):
    kc = K // 128
    DT = F32 if f32 else BF16
    with p.scope():
        wb = p.sb(tag + 'wb', [128, kc, N], DT)
        wst = [p.sb(tag + 'wst%d' % i, [128, N]) for i in range(2)]
        for c in range(kc):
            if f32:
                p.dma(wb[:, c, :], w[c * 128:(c + 1) * 128, :], q=('sp', 'act')[c % 2])
            else:
                p.dma(wst[c % 2][:], w[c * 128:(c + 1) * 128, :], q=('sp', 'act')[c % 2])
                p.cp(wb[:, c, :], wst[c % 2][:], eng=('dve', 'pool')[c % 2])
        xs = [p.sb(tag + 'xs%d' % i, [128, K]) for i in range(2)]
        xT = [p.sb(tag + 'xT%d' % i, [128, kc, 128], DT) for i in range(2)]
        ys = [p.sb(tag + 'ys%d' % i, [128, 512]) for i in range(2)]
        pt = [p.ps(tag + 'pt%d' % i, [128, 512]) for i in range(2)]
        pa = [p.ps(tag + 'pa%d' % i, [128, 512]) for i in range(4)]
        nb = 0
        for t in range(ntok // 128):
            X = xs[t % 2]
            p.dma(X[:], x[t * 128:(t + 1) * 128, :], q='sp')
            for g in range(0, kc, 4):
                n4 = min(4, kc - g)
                P = pt[(g // 4) % 2]
                for j in range(n4):
                    src = X[:, (g + j) * 128:(g + j + 1) * 128]
                    dst = P[:, j * 128:(j + 1) * 128]
                    p.op('pe', lambda e, dst=dst, src=src: e.transpose(dst, src, ident), [src, ident], [dst])
                p.cp(xT[t % 2][:, g:g + n4, :], P[:, 0:n4 * 128].rearrange('p (a b) -> p a b', a=n4), eng='act')
            for n0 in range(0, N, 512):
                nn = min(512, N - n0)
                A = pa[nb % 4]
                Y = ys[nb % 2]
                for c in range(kc):
                    p.mm(A[:, 0:nn], xT[t % 2][:, c, :], wb[:, c, n0:n0 + nn], start=(c == 0), stop=(c == kc - 1))
                if relu2:
                    p.act(Y[:, 0:nn], A[:, 0:nn], AF.Relu)
                    p.tt(Y[:, 0:nn], Y[:, 0:nn], Y[:, 0:nn], ALU.mult, eng='pool')
                else:
                    p.cp(Y[:, 0:nn], A[:, 0:nn], eng=('act', 'dve')[nb % 2])
                p.dma(y[t * 128:(t + 1) * 128, n0:n0 + nn], Y[:, 0:nn], q='pool')
                nb += 1


def ln_program(p, xa, xb, gv, bv, out, ntok, alpha, tag='n'):
    with p.scope():
        g = p.bc(tag + 'g', gv, D)
        b = p.bc(tag + 'b', bv, D)
        A = [p.sb(tag + 'a%d' % i, [128, D]) for i in range(2)]
        B = [p.sb(tag + 'b%d' % i, [128, D]) for i in range(2)]
        st = p.sb(tag + 'st', [128, 12])
        mv = p.sb(tag + 'mv', [128, 2])
        rs = p.sb(tag + 'rs', [128, 1])
        for t in range(ntok // 128):
            rows = slice(t * 128, (t + 1) * 128)
            a, bb = A[t % 2], B[t % 2]
            p.dma(a[:], xa[rows, :])
            p.dma(bb[:], xb[rows, :], q='act')
            p.stt(a[:], a[:], float(alpha), bb[:], ALU.mult, ALU.add)
            p.op('dve', lambda e, a=a: e.bn_stats(st[:, 0:6], a[:, 0:512]), [a[:, 0:512]], [st[:, 0:6]])
            p.op('dve', lambda e, a=a: e.bn_stats(st[:, 6:12], a[:, 512:1024]), [a[:, 512:1024]], [st[:, 6:12]])
            p.op('dve', lambda e: e.bn_aggr(mv[:], st[:]), [st[:]], [mv[:]])
            p.rsqrt(rs[:], mv[:, 1:2], LN_EPS)
            p.ts(a[:], a[:], mv[:, 0:1], ALU.subtract, rs[:], ALU.mult)
            p.tt(a[:], a[:], g[:], ALU.mult)
            p.tt(a[:], a[:], b[:], ALU.add, eng='pool')
            p.dma(out[rows, :], a[:], q='pool')


def head_norm(p, out, x, H, d, eps, tmp, stat, center=False):
    x3, o3, t3 = _h3(x, H), _h3(out, H), _h3(tmp, H)
    mean, ms = stat[:, 0:H], stat[:, H:2 * H]
    if center:
        p.red(mean, x3)
        p.ts(mean, mean, 1.0 / d, ALU.mult)
        p.tt(o3, x3, _bcol(mean, H, d), ALU.subtract)
        src = o3
    else:
        src = x3
    p.tt(t3, src, src, ALU.mult)
    p.red(ms, t3)
    p.ts(ms, ms, 1.0 / d, ALU.mult)
    p.rsqrt(ms, ms, eps)
    p.tt(o3, src, _bcol(ms, H, d), ALU.mult)


def prep_odd(p, ident, projh, prm, outs, ntok):
    with p.scope():
        l0 = p.bc('pl0', prm['hg_lower'][0, :], 512)
        lb = p.bc('plb', prm['hg_lower'][1, :], 512)
        p.tt(lb[:], lb[:], l0[:], ALU.subtract)
        p.act(lb[:], lb[:], AF.Sigmoid)
        oml = p.sb('poml', [128, 512])
        p.ts(oml[:], lb[:], -1.0, ALU.mult, 1.0, ALU.add)
        mu0 = p.bc('pmu0', prm['rw_mu'][0, :], 1792)
        mu1 = p.bc('pmu1', prm['rw_mu'][1, :], 1792)
        w0 = [p.bc('pw0%d' % d, prm['rw_w0'][d, :], 512) for d in range(2)]
        a0 = p.bc('pa0', prm['rw_a0'], 512)
        k_k = p.bc('pkk', prm['rw_k_k'], 512)
        k_a = p.bc('pka', prm['rw_k_a'], 512)
        w2 = [p.sb('pw2%d' % d, [64, 512]) for d in range(2)]
        a2 = p.sb('pa2', [64, 512])
        g2 = p.sb('pg2', [128, 512])
        for d in range(2):
            p.dma(w2[d][:], prm['rw_w2'][d, :, :])
        p.dma(a2[:], prm['rw_a2'])
        p.dma(g2[:], prm['rw_g2'])
        hin = [p.sb('phin%d' % i, [128, 2560]) for i in range(2)]
        cur = [p.sb('pcur%d' % i, [128, 1792]) for i in range(2)]
        prv = [p.sb('pprv%d' % i, [128, 1792]) for i in range(2)]
        nxt = [p.sb('pnxt%d' % i, [128, 1792]) for i in range(2)]
        hq = p.sb('phq', [128, 512])
        hf = p.sb('phf', [128, 1024])
        hk = p.sb('phk', [128, 1024])
        hlw = p.sb('phlw', [128, 1024])
        lo = p.sb('plo', [128, 256])
        loT = p.sb('ploT', [128, 3, 128])
        rlw = p.sb('prlw', [128, 1024])
        ra = p.sb('pra', [128, 512])
        rg = p.sb('prg', [128, 512])
        kq = p.sb('pkq', [128, 512])
        kkn = p.sb('pkkn', [128, 512])
        tmp = p.sb('ptmp', [128, 512])
        kp = p.sb('pkp', [128, 512])
        kav = p.sb('pkav', [128, 512])
        stat = p.sb('pstat', [128, 16])
        pp = [p.ps('ppp%d' % i, [128, 512]) for i in range(4)]
        for t in range(ntok // 128):
            r0 = t * 128
            rows = slice(r0, r0 + 128)
            H_, C_, P_, N_ = hin[t % 2], cur[t % 2], prv[t % 2], nxt[t % 2]
            p.dma(H_[:], projh[r0 + 1:r0 + 129, 0:2560])
            p.dma(C_[:], projh[r0 + 1:r0 + 129, 2560:4352], q='act')
            p.dma(P_[:], projh[r0:r0 + 128, 2560:4352])
            p.dma(N_[:], projh[r0 + 2:r0 + 130, 2560:4352], q='act')
            p.act(hq[:], H_[:, 0:512], AF.Silu)
            p.dma(outs['hq'][rows, :], hq[:], q='pool')
            p.act(hf[:], H_[:, 512:1536], AF.Sigmoid)
            for d in range(2):
                sl = slice(512 * d, 512 * (d + 1))
                p.tt(hf[:, sl], hf[:, sl], oml[:], ALU.mult)
                p.tt(hf[:, sl], hf[:, sl], lb[:], ALU.add, eng='pool')
            p.ts(hk[:], hf[:], -1.0, ALU.mult, 1.0, ALU.add)
            p.act(hlw[:], hf[:], AF.Ln)
            p.dma(outs['hk'][rows, :], hk[:], q='pool')
            p.dma(outs['hlw'][rows, :], hlw[:], q='pool')
            p.tt(P_[:], P_[:], C_[:], ALU.subtract)
            p.tt(P_[:], P_[:], mu0[:], ALU.mult)
            p.tt(N_[:], N_[:], C_[:], ALU.subtract, eng='pool')
            p.tt(N_[:], N_[:], mu1[:], ALU.mult, eng='pool')
            p.tt(C_[:], C_[:], P_[:], ALU.add)
            p.tt(C_[:], C_[:], N_[:], ALU.add)
            r, k, v = C_[:, 0:512], C_[:, 512:1024], C_[:, 1024:1536]
            p.dma(outs['rr'][rows, :], r, q='pool')
            p.dma(outs['rv'][rows, :], v, q='pool')
            p.act(lo[:, 0:64], C_[:, 1536:1600], AF.Tanh)
            p.cp(lo[:, 64:128], C_[:, 1600:1664])
            p.act(lo[:, 128:256], C_[:, 1664:1792], AF.Sigmoid)
            p.tr(pp[0][0:64, 0:128], lo[:, 0:64], ident)
            p.tr(pp[0][0:64, 128:256], lo[:, 64:128], ident)
            p.tr(pp[0][:, 256:384], lo[:, 128:256], ident)
            p.cp(loT[0:64, 0:2, :], pp[0][0:64, 0:256].rearrange('p (a b) -> p a b', a=2), eng='act')
            p.cp(loT[:, 2, :], pp[0][:, 256:384], eng='act')
            for d in range(2):
                p.mm(pp[1 + d][:, :], loT[0:64, 0, :], w2[d][:])
                sl = slice(512 * d, 512 * (d + 1))
                p.tt(rlw[:, sl], pp[1 + d][:, :], w0[d][:], ALU.add)
            p.act(rlw[:], rlw[:], AF.Sigmoid)
            p.ts(rlw[:], rlw[:], -float(np.exp(-0.5)), ALU.mult)
            p.dma(outs['rlw'][rows, :], rlw[:], q='pool')
            p.mm(pp[3][:, :], loT[0:64, 1, :], a2[:])
            p.tt(ra[:], pp[3][:, :], a0[:], ALU.add)
            p.act(ra[:], ra[:], AF.Sigmoid)
            p.mm(pp[1][:, :], loT[:, 2, :], g2[:])
            p.cp(rg[:], pp[1][:, :], eng='act')
            p.dma(outs['rg'][rows, :], rg[:], q='pool')
            p.tt(kq[:], k, k_k[:], ALU.mult)
            p.tt(_h3(tmp[:], 8), _h3(kq[:], 8), _h3(kq[:], 8), ALU.mult)
            p.red(stat[:, 0:8], _h3(tmp[:], 8))
            p.rsqrt(stat[:, 0:8], stat[:, 0:8], 1e-6)
            p.tt(_h3(kkn[:], 8), _h3(kq[:], 8), _bcol(stat[:, 0:8], 8, 64), ALU.mult)
            p.stt(tmp[:], ra[:], -1.0, k_a[:], ALU.add, ALU.mult)
            p.ts(tmp[:], tmp[:], 1.0, ALU.add)
            p.tt(kp[:], k, tmp[:], ALU.mult)
            p.tt(kav[:], kkn[:], ra[:], ALU.mult, eng='pool')
            p.dma(outs['rk'][rows, :], kp[:], q='pool')
            p.dma(outs['rkk'][rows, :], kkn[:], q='pool')
            p.dma(outs['rka'][rows, :], kav[:], q='pool')


def post_odd(p, ins, prm, ymix, ntok):
    with p.scope():
        nw = p.sb('qnw', [128, 512])
        for h in range(4):
            p.dma(nw[:, 128 * h:128 * (h + 1)], prm['hg_norm_w'].rearrange('(o f) -> o f', o=1).broadcast_to([128, 128]), q='act')
        lw_ = p.bc('qlw', prm['rw_lnx_w'], 512)
        lb_ = p.bc('qlb', prm['rw_lnx_b'], 512)
        rk_ = p.bc('qrk', prm['rw_r_k'], 512)
        names = ['of', 'ob', 'hg', 'yf', 'yb', 'rr', 'rk', 'rv', 'rg']
        X = [{n: p.sb('q%s%d' % (n, i), [128, 512]) for n in names} for i in range(2)]
        tmp = p.sb('qtmp', [128, 512])
        o = p.sb('qo', [128, 512])
        y = p.sb('qy', [128, 512])
        stat = p.sb('qstat', [128, 16])
        yo = [p.sb('qyo%d' % i, [128, 1024]) for i in range(2)]
        for t in range(ntok // 128):
            rows = slice(t * 128, (t + 1) * 128)
            x = X[t % 2]
            Y = yo[t % 2]
            for i, n in enumerate(names):
                p.dma(x[n][:], ins[n][rows, :], q=('sp', 'act')[i % 2])
            p.tt(o[:], x['of'][:], x['ob'][:], ALU.add)
            head_norm(p, o[:], o[:], 4, 128, 1e-6, tmp[:], stat[:])
            p.tt(o[:], o[:], nw[:], ALU.mult)
            p.act(x['hg'][:], x['hg'][:], AF.Silu)
            p.tt(Y[:, 0:512], o[:], x['hg'][:], ALU.mult)
            p.tt(y[:], x['yf'][:], x['yb'][:], ALU.add)
            head_norm(p, y[:], y[:], 8, 64, 64e-5, tmp[:], stat[:], center=True)
            p.tt(y[:], y[:], lw_[:], ALU.mult)
            p.tt(y[:], y[:], lb_[:], ALU.add, eng='pool')
            p.tt(tmp[:], x['rr'][:], x['rk'][:], ALU.mult)
            p.tt(tmp[:], tmp[:], rk_[:], ALU.mult)
            p.red(stat[:, 0:8], _h3(tmp[:], 8))
            p.tt(_h3(tmp[:], 8), _h3(x['rv'][:], 8), _bcol(stat[:, 0:8], 8, 64), ALU.mult)
            p.tt(y[:], y[:], tmp[:], ALU.add)
            p.tt(Y[:, 512:1024], y[:], x['rg'][:], ALU.mult)
            p.dma(ymix[rows, :], Y[:], q='pool')


PROFILE = None


def f32c(a):
    return np.ascontiguousarray(np.asarray(a, dtype=np.float32))


def launch(body, in_maps, out_shapes, internal=None):
    n = len(in_maps)
    p = Prog()
    ins = {k: p.dram(k, v.shape, F32, 'ExternalInput') for k, v in in_maps[0].items()}
    outs = {k: p.dram(k, shp, F32, 'ExternalOutput') for k, shp in out_shapes.items()}
    scr = {k: p.dram(k, shp, F32, 'Internal') for k, shp in (internal or {}).items()}
    body(p, ins, outs, scr)
    p.barrier()
    p.close()
    res = run_bass_kernel_spmd(p.nc, in_maps, core_ids=list(range(n)))
    if PROFILE is not None:
        PROFILE.append((p.nops, getattr(res, 'exec_time_ns', None)))
    return res.results


def load_ident(p, ins):
    ident = p.sb('identc', [128, 128])
    p.dma(ident[:], ins['ident'])
    return ident[:]


def shard_halo(x2, B, T, n, halo=1, pad_to=128):
    ntok = B * T // n
    out = []
    for c in range(n):
        lo = c * ntok
        a = np.zeros((ntok + pad_to, x2.shape[1]), np.float32)
        a[halo:halo + ntok] = x2[lo:lo + ntok]
        for h in range(1, halo + 1):
            if (lo - h) // T == lo // T and lo - h >= 0:
                a[halo - h] = x2[lo - h]
            hi = lo + ntok - 1 + h
            if hi < B * T and hi // T == (lo + ntok - 1) // T:
                a[halo + ntok - 1 + h] = x2[hi]
        out.append(a)
    return out


def run_recur(units_a, cfg_a, units_b, cfg_b, T, n):
    consts = recur_consts()
    na, nb = len(units_a) // n, len(units_b) // n
    in_maps = []
    for c in range(n):
        m = {'rconst': consts}
        if na:
            m['ua'] = f32c(np.stack(units_a[c * na:(c + 1) * na]))
        if nb:
            m['ub'] = f32c(np.stack(units_b[c * nb:(c + 1) * nb]))
        in_maps.append(m)
    outsh = {}
    if na:
        outsh['ya'] = (na, T, cfg_a[1])
    if nb:
        outsh['yb'] = (nb, T, cfg_b[1])

    def body(p, ins, outs, scr):
        for key, cnt, cfg, yk in (('ua', na, cfg_a, 'ya'), ('ub', nb, cfg_b, 'yb')):
            if not cnt:
                continue
            dk, dv, delta = cfg
            nin = 6 if delta else 4
            with p.scope():
                RA = Recur(p, ins['rconst'], dk, dv, delta, tag=key + 'a')
                RB = Recur(p, ins['rconst'], dk, dv, delta, tag=key + 'b')
                for u in range(0, cnt, 2):
                    RA.reset()
                    RB.reset()
                    for it in range(T // 128):
                        rows = slice(it * 128, (it + 1) * 128)
                        gens = []
                        for R_, uu in ((RA, u), (RB, u + 1)):
                            if uu < cnt:
                                srcs = [ins[key][uu, i, rows, 0:(dv if i == 2 else dk)] for i in range(nin)]
                                gens.append(R_.tile_gen(it, srcs, outs[yk][uu, rows, :]))
                        for _ in itertools.zip_longest(*gens):
                            pass
    res = launch(body, in_maps, outsh)
    ya = [res[c]['ya'][u] for c in range(n) for u in range(na)] if na else []
    yb = [res[c]['yb'][u] for c in range(n) for u in range(nb)] if nb else []
    return ya, yb


def dense_tail(p, ident, x, ymix, w, outx, scr, ntok, li):
    linear_program(p, ident, ymix, w['w_out'], scr['mix'], ntok, D, D, tag='lo')
    ln_program(p, x, scr['mix'], w['ln_g'][0, :], w['ln_b'][0, :], scr['x1'], ntok, ALPHA, tag='na')
    linear_program(p, ident, scr['x1'], w['mlp_w1'], scr['hmid'], ntok, D, 4 * D, relu2=True, tag='l1')
    linear_program(p, ident, scr['hmid'], w['mlp_w2'], scr['mlp'], ntok, 4 * D, D, tag='l2')
    ln_program(p, scr['x1'], scr['mlp'], w['ln_g'][1, :], w['ln_b'][1, :], outx, ntok, ALPHA, tag='nb')


def layer_odd(x2, W, B, T, n):
    ntok = B * T // n
    ident = np.eye(128, dtype=np.float32)
    xh = shard_halo(x2, B, T, n)
    pk = ['hg_lower', 'rw_mu', 'rw_w0', 'rw_w2', 'rw_a0', 'rw_a2', 'rw_g2', 'rw_k_k', 'rw_k_a']
    in_maps = [dict({'xh': xh[c], 'w_in': f32c(W['w_in']), 'ident': ident}, **{k: f32c(W[k]) for k in pk}) for c in range(n)]
    onames = {'hq': 512, 'hk': 1024, 'hlw': 1024, 'hv': 512, 'hg': 512, 'rr': 512, 'rk': 512, 'rv': 512,
              'rkk': 512, 'rka': 512, 'rlw': 1024, 'rg': 512}

    def body1(p, ins, outs, scr):
        idn = load_ident(p, ins)
        linear_program(p, idn, ins['xh'], ins['w_in'], scr['projh'], ntok + 128, D, 4352, tag='li')
        p.dma(outs['hv'], scr['projh'][1:ntok + 1, 1536:2048])
        p.dma(outs['hg'], scr['projh'][1:ntok + 1, 2048:2560], q='act')
        prep_odd(p, idn, scr['projh'], ins, outs, ntok)
    r1 = launch(body1, in_maps, {k: (ntok, v) for k, v in onames.items()}, {'projh': (ntok + 128, 4352)})
    full = {k: np.concatenate([r1[c][k] for c in range(n)], 0).reshape(B, T, -1) for k in onames}

    def dirv(a, z):
        return a[:, ::-1] if z else a
    ua, ub = [], []
    for b in range(B):
        for z in range(2):
            for h in range(4):
                hs = slice(128 * h, 128 * (h + 1))
                zs = slice(512 * z + 128 * h, 512 * z + 128 * (h + 1))
                arrs = [full['hq'][b:b + 1, :, hs], full['hk'][b:b + 1, :, zs], full['hv'][b:b + 1, :, hs], full['hlw'][b:b + 1, :, zs]]
                ua.append(np.stack([dirv(a, z)[0] for a in arrs]))
            for h in range(8):
                hs = slice(64 * h, 64 * (h + 1))
                zs = slice(512 * z + 64 * h, 512 * z + 64 * (h + 1))
                arrs = [full['rr'][b:b + 1, :, hs], full['rk'][b:b + 1, :, hs], full['rv'][b:b + 1, :, hs],
                        full['rlw'][b:b + 1, :, zs], full['rkk'][b:b + 1, :, hs], full['rka'][b:b + 1, :, hs]]
                ub.append(np.stack([dirv(a, z)[0] for a in arrs]))
    ya, yb = run_recur(ua, (128, 128, False), ub, (64, 64, True), T, n)
    o = np.zeros((2, B, T, 512), np.float32)
    y = np.zeros((2, B, T, 512), np.float32)
    ia = ib = 0
    for b in range(B):
        for z in range(2):
            for h in range(4):
                o[z, b, :, 128 * h:128 * (h + 1)] = ya[ia][::-1] if z else ya[ia]
                ia += 1
            for h in range(8):
                y[z, b, :, 64 * h:64 * (h + 1)] = yb[ib][::-1] if z else yb[ib]
                ib += 1

    def sh(a):
        return np.split(f32c(a.reshape(B * T, -1)), n, 0)
    parts = {'x': sh(x2), 'of': sh(o[0]), 'ob': sh(o[1]), 'yf': sh(y[0]), 'yb': sh(y[1])}
    for k in ('hg', 'rr', 'rk', 'rv', 'rg'):
        parts[k] = sh(full[k])
    wk = ['w_out', 'ln_g', 'ln_b', 'mlp_w1', 'mlp_w2', 'hg_norm_w', 'rw_lnx_w', 'rw_lnx_b', 'rw_r_k']
    in_maps = [dict({k: f32c(v[c]) for k, v in parts.items()}, ident=ident, **{k: f32c(W[k]) for k in wk}) for c in range(n)]

    def body3(p, ins, outs, scr):
        idn = load_ident(p, ins)
        post_odd(p, ins, ins, scr['ymix'], ntok)
        dense_tail(p, idn, ins['x'], scr['ymix'], ins, outs['xo'], scr, ntok, 1)
    r3 = launch(body3, in_maps, {'xo': (ntok, D)},
                {'ymix': (ntok, D), 'mix': (ntok, D), 'x1': (ntok, D), 'hmid': (ntok, 4 * D), 'mlp': (ntok, D)})
    return np.concatenate([r3[c]['xo'] for c in range(n)], 0)


def prep_even(p, projh, prm, outs, ntok):
    with p.scope():
        hw = [p.bc('ehw%d' % j, prm['hy_conv_w'][j, :], 1536) for j in range(3)]
        hbias = p.bc('ehb', prm['hy_conv_b'], 1536)
        gw = [p.bc('egw%d' % j, prm['gdn_conv_w'][j, :], 1536) for j in range(5)]
        nA = p.bc('enA', prm['gdn_a_log'], 8)
        p.act(nA[:], nA[:], AF.Exp)
        p.ts(nA[:], nA[:], -1.0, ALU.mult)
        dtb = p.bc('edtb', prm['gdn_dt_bias'], 8)
        xin = [p.sb('exin%d' % j, [128, 1536]) for j in range(5)]
        acc = p.sb('eacc', [128, 1536])
        acc2 = p.sb('eacc2', [128, 1536])
        tmp = p.sb('etmp', [128, 1536])
        zabw = p.sb('ezab', [128, 128])
        zab = zabw[:, 116:128]
        stat = p.sb('estat', [128, 16])
        lg = p.sb('elg', [128, 8])
        gg = p.sb('egg', [128, 8])
        beta = p.sb('ebeta', [128, 4])
        kb = p.sb('ekb', [128, 512])
        ka = p.sb('eka', [128, 1024])
        lw = p.sb('elw', [128, 1024])
        for t in range(ntok // 128):
            r0 = t * 128
            rows = slice(r0, r0 + 128)
            for j in range(3):
                p.dma(xin[j][:], projh[r0 + 1 + j:r0 + 129 + j, 0:1536], q=('sp', 'act')[j % 2])
            p.tt(acc[:], xin[0][:], hw[0][:], ALU.mult)
            for j in (1, 2):
                p.tt(tmp[:], xin[j][:], hw[j][:], ALU.mult, eng='pool')
                p.tt(acc[:], acc[:], tmp[:], ALU.add)
            p.tt(acc[:], acc[:], hbias[:], ALU.add)
            p.dma(outs['hx1'][rows, :], acc[:, 0:512], q='pool')
            p.dma(outs['hx2'][rows, :], acc[:, 512:1024], q='pool')
            p.dma(outs['hv'][rows, :], acc[:, 1024:1536], q='pool')
            for j in range(5):
                p.dma(xin[j][:], projh[r0 + j:r0 + 128 + j, 1536:3072], q=('sp', 'act')[j % 2])
            p.dma(zabw[:], projh[r0 + 2:r0 + 130, 3468:3596])
            p.tt(acc2[:], xin[0][:], gw[0][:], ALU.mult)
            for j in range(1, 5):
                p.tt(tmp[:], xin[j][:], gw[j][:], ALU.mult, eng='pool')
                p.tt(acc2[:], acc2[:], tmp[:], ALU.add)
            p.act(acc2[:], acc2[:], AF.Silu)
            q, k, v = acc2[:, 0:512], acc2[:, 512:1024], acc2[:, 1024:1536]
            p.dma(outs['gv'][rows, :], v, q='pool')
            for i, (src, scale) in enumerate(((q, 128.0 ** -0.5), (k, 1.0))):
                s8 = stat[:, 4 * i:4 * i + 4]
                p.tt(_h3(tmp[:, 0:512], 4), _h3(src, 4), _h3(src, 4), ALU.mult)
                p.red(s8, _h3(tmp[:, 0:512], 4))
                p.rsqrt(s8, s8, 1e-6)
                if scale != 1.0:
                    p.ts(s8, s8, float(scale), ALU.mult)
                p.tt(_h3(src, 4), _h3(src, 4), _bcol(s8, 4, 128), ALU.mult)
            p.dma(outs['gq'][rows, :], q, q='pool')
            p.dma(outs['gkk'][rows, :], k, q='pool')
            p.act(beta[:], zab[:, 8:12], AF.Sigmoid)
            p.tt(lg[:], zab[:, 0:8], dtb[:], ALU.add)
            p.act(lg[:], lg[:], AF.Exp)
            p.act(lg[:], lg[:], AF.Ln, bias=1.0)
            p.tt(lg[:], lg[:], nA[:], ALU.mult)
            p.act(gg[:], lg[:], AF.Exp)
            p.tt(_h3(kb[:], 4), _h3(k, 4), _bcol(beta[:], 4, 128), ALU.mult)
            p.dma(outs['gkb'][rows, :], kb[:], q='pool')
            for d in range(2):
                sl = slice(512 * d, 512 * (d + 1))
                p.tt(_h3(ka[:, sl], 4), _h3(kb[:], 4), _bcol(gg[:, 4 * d:4 * d + 4], 4, 128), ALU.mult)
                p.cp(_h3(lw[:, sl], 4), _bcol(lg[:, 4 * d:4 * d + 4], 4, 128), eng='pool')
            p.dma(outs['gka'][rows, :], ka[:], q='pool')
            p.dma(outs['glw'][rows, :], lw[:], q='pool')


def post_even(p, ins, prm, ymix, ntok):
    with p.scope():
        nw = p.sb('rnw', [128, 512])
        for h in range(4):
            p.dma(nw[:, 128 * h:128 * (h + 1)], prm['gdn_norm_w'].rearrange('(o f) -> o f', o=1).broadcast_to([128, 128]), q='act')
        names = ['of', 'ob', 'zg', 'ya']
        X = [{n: p.sb('r%s%d' % (n, i), [128, 512]) for n in names} for i in range(2)]
        tmp = p.sb('rtmp', [128, 512])
        o = p.sb('ro', [128, 512])
        stat = p.sb('rstat', [128, 16])
        yo = [p.sb('ryo%d' % i, [128, 1024]) for i in range(2)]
        for t in range(ntok // 128):
            rows = slice(t * 128, (t + 1) * 128)
            x, Y = X[t % 2], yo[t % 2]
            for i, n in enumerate(names):
                p.dma(x[n][:], ins[n][rows, :], q=('sp', 'act')[i % 2])
            p.cp(Y[:, 0:512], x['ya'][:], eng='pool')
            p.tt(o[:], x['of'][:], x['ob'][:], ALU.add)
            head_norm(p, o[:], o[:], 4, 128, 1e-6, tmp[:], stat[:])
            p.tt(o[:], o[:], nw[:], ALU.mult)
            p.act(x['zg'][:], x['zg'][:], AF.Silu)
            p.tt(Y[:, 512:1024], o[:], x['zg'][:], ALU.mult)
            p.dma(ymix[rows, :], Y[:], q='pool')


def filter_consts(L):
    pos = np.arange(L, dtype=np.float32)[:, None]
    t = pos / np.float32(max(L - 1, 1))
    bands = np.linspace(1e-4, 15, 16, dtype=np.float32)[None, :]
    ang = bands * np.float32(2.0 * np.pi / L) * pos
    z = np.concatenate([t, np.cos(ang), -np.sin(ang)], -1).astype(np.float32)
    max_decay = np.log(1e-2) / 0.3
    min_decay = np.log(1e-2) / 1.5
    deltas = np.abs(np.linspace(min_decay, max_decay, 512, dtype=np.float32))
    window = np.exp(-t * deltas).astype(np.float32)
    return z, window


def filter_program(p, ins, outs, npos):
    with p.scope():
        zT = p.sb('fzT', [33, npos])
        p.dma(zT[:], ins['zT'])
        w1 = p.sb('fw1', [33, 64])
        p.dma(w1[:], ins['f_w1'])
        w2 = p.sb('fw2', [64, 64])
        p.dma(w2[:], ins['f_w2'], q='act')
        w3 = p.sb('fw3', [64, 64])
        p.dma(w3[:], ins['f_w3'])
        w4 = p.sb('fw4', [64, 2048])
        p.dma(w4[:], ins['f_w4'], q='act')
        bf = p.sb('fbf', [64, 128])
        p.dma(bf[:], ins['f_bf'])
        hT = [p.sb('fhT%d' % i, [64, npos]) for i in range(3)]
        arg = p.sb('farg', [64, npos])
        wm1 = p.sb('fwm1', [64, npos])
        wm2 = p.sb('fwm2', [64, npos])
        win = p.sb('fwin', [128, 512])
        ho = [p.sb('fho%d' % i, [128, 512]) for i in range(2)]
        pf = [p.ps('fpf%d' % i, [128, 512]) for i in range(4)]
        srcs = [(w1, zT, 33), (w2, hT[0], 64), (w3, hT[1], 64)]
        for li, (w, src, K) in enumerate(srcs):
            for c0 in range(0, npos, 512):
                cs = slice(c0, min(npos, c0 + 512))
                P = pf[li % 2][0:64, 0:cs.stop - cs.start]
                p.mm(P, w[0:K, :], src[0:K, cs])
                p.ts(arg[:, cs], P, bf[:, li:li + 1], ALU.add, bf[:, 3:4], ALU.mult)
                p.wrap(arg[:, cs], arg[:, cs], wm1[:, cs], wm2[:, cs])
                p.act(hT[li][:, cs], arg[:, cs], AF.Sin)
        nb = 0
        for t in range(npos // 128):
            rows = slice(t * 128, (t + 1) * 128)
            p.dma(win[:], ins['window'][rows, :])
            for blk in range(4):
                P = pf[2 + nb % 2]
                Y = ho[nb % 2]
                p.mm(P[:, :], hT[2][:, rows], w4[:, 512 * blk:512 * (blk + 1)])
                p.tt(Y[:], P[:, :], win[:], ALU.mult)
                p.dma(outs['h'][rows, 512 * blk:512 * (blk + 1)], Y[:], q='pool')
                nb += 1


def fft_consts():
    c = np.zeros((128, 800), np.float64)
    a = np.arange(64)[:, None]
    k1 = np.arange(64)[None, :]
    th = 2 * np.pi * a * k1 / 64
    c[0:64, 0:64] = np.cos(th)
    c[0:64, 64:128] = -np.sin(th)
    pp = np.arange(128)[:, None]
    tw = 2 * np.pi * pp * k1 / 8192
    c[:, 128:192] = np.cos(tw)
    c[:, 192:256] = np.sin(tw)
    k2 = np.arange(128)[None, :]
    t2 = 2 * np.pi * pp * k2 / 128
    c[:, 256:384] = np.cos(t2)
    c[:, 384:512] = np.sin(t2)
    c[:, 512:640] = -np.sin(t2)
    a32 = np.arange(32)[None, :]
    k1c = np.arange(64)[:, None]
    thi = 2 * np.pi * a32 * k1c / 64
    c[0:64, 640:672] = np.cos(thi) / 8192
    c[64:128, 640:672] = -np.sin(thi) / 8192
    c[:, 672:800] = np.eye(128)
    return c.astype(np.float32)


RC = 16


class HyenaConv:
    def __init__(self, p, cdram):
        self.p = p
        c = p.sb('hc', [128, 800])
        p.dma(c[:], cdram)
        self.F1, self.Tc, self.Ts = c[0:64, 0:128], c[:, 128:192], c[:, 192:256]
        self.C2, self.S2, self.nS2 = c[:, 256:384], c[:, 384:512], c[:, 512:640]
        self.F1i, self.ident = c[:, 640:672], c[:, 672:800]
        sb, ps = p.sb, p.ps
        self.yre, self.yim = sb('hyre', [128, RC, 64]), sb('hyim', [128, RC, 64])
        self.t1, self.t2 = sb('ht1', [128, RC, 64]), sb('ht2', [128, RC, 64])
        self.r2 = sb('hr2', [128, RC, 128])
        self.rT = sb('hrT', [128, RC, 128])
        self.pA = [ps('hpA%d' % i, [128, 512]) for i in range(2)]
        self.pB = [ps('hpB%d' % i, [128, 512]) for i in range(4)]
        self.pC = [ps('hpC%d' % i, [128, 512]) for i in range(2)]

    def _tw(self, ore, oim, re, im, sign):
        p = self.p
        Tc = self.Tc.unsqueeze(1).broadcast_to([128, RC, 64])
        Ts = self.Ts.unsqueeze(1).broadcast_to([128, RC, 64])
        t1, t2 = self.t1[:], self.t2[:]
        p.tt(t1, re, Tc, ALU.mult)
        p.tt(t2, im, Ts, ALU.mult, eng='pool')
        p.tt(ore, t1, t2, ALU.add if sign > 0 else ALU.subtract)
        p.tt(t1, im, Tc, ALU.mult)
        p.tt(t2, re, Ts, ALU.mult, eng='pool')
        p.tt(oim, t1, t2, ALU.subtract if sign > 0 else ALU.add)

    def fwd(self, X, K, dre, dim):
        p = self.p
        for r in range(RC):
            P = self.pA[(r // 4) % 2]
            p.mm(P[:, (r % 4) * 128:(r % 4 + 1) * 128], X[0:K, r, :], self.F1[0:K, :])
            if r % 4 == 3:
                P3 = P[:, :].rearrange('p (r c) -> p r c', r=4)
                p.cp(self.yre[:, r - 3:r + 1, :], P3[:, :, 0:64], eng='act')
                p.cp(self.yim[:, r - 3:r + 1, :], P3[:, :, 64:128], eng='dve')
        self._tw(self.r2[:, :, 0:64], self.r2[:, :, 64:128], self.yre[:], self.yim[:], +1)
        for cb in range(RC // 8):
            rs = slice(8 * cb, 8 * cb + 8)
            re, im = self.r2[:, rs, 0:64], self.r2[:, rs, 64:128]
            Pre, Pim = self.pB[(2 * cb) % 4], self.pB[(2 * cb + 1) % 4]
            p.mm(Pre[:, :], self.C2, re, start=True, stop=False)
            p.mm(Pre[:, :], self.S2, im, start=False, stop=True)
            p.mm(Pim[:, :], self.C2, im, start=True, stop=False)
            p.mm(Pim[:, :], self.nS2, re, start=False, stop=True)
            p.cp(dre[:, rs, :], Pre[:, :].rearrange('p (r c) -> p r c', r=8), eng='act')
            p.cp(dim[:, rs, :], Pim[:, :].rearrange('p (r c) -> p r c', r=8), eng='dve')

    def mul(self, ore, oim, are, aim, bre, bim):
        p = self.p
        t1, t2 = self.t1[:], self.t2[:]
        p.tt(t1, are, bre, ALU.mult)
        p.tt(t2, aim, bim, ALU.mult, eng='pool')
        p.tt(ore, t1, t2, ALU.subtract)
        p.tt(t1, are, bim, ALU.mult)
        p.tt(t2, aim, bre, ALU.mult, eng='pool')
        p.tt(oim, t1, t2, ALU.add)

    def inv(self, pre, pim, epilogue):
        p = self.p
        for cb in range(RC // 8):
            rs = slice(8 * cb, 8 * cb + 8)
            re, im = pre[:, rs, :], pim[:, rs, :]
            Pre, Pim = self.pB[(2 * cb) % 4], self.pB[(2 * cb + 1) % 4]
            p.mm(Pre[:, :], self.C2, re, start=True, stop=False)
            p.mm(Pre[:, :], self.nS2, im, start=False, stop=True)
            p.mm(Pim[:, :], self.S2, re, start=True, stop=False)
            p.mm(Pim[:, :], self.C2, im, start=False, stop=True)
            p.cp(self.yre[:, rs, :], Pre[:, :].rearrange('p (r c) -> p r c', r=8), eng='act')
            p.cp(self.yim[:, rs, :], Pim[:, :].rearrange('p (r c) -> p r c', r=8), eng='dve')
        self._tw(self.r2[:, :, 0:64], self.r2[:, :, 64:128], self.yre[:], self.yim[:], -1)
        for r in range(RC):
            P = self.pA[(r // 4) % 2]
            dst = P[:, (r % 4) * 128:(r % 4 + 1) * 128]
            src = self.r2[:, r, :]
            p.op('pe', lambda e, dst=dst, src=src: e.transpose(dst, src, self.ident), [src, self.ident], [dst])
            if r % 4 == 3:
                p.cp(self.rT[:, r - 3:r + 1, :], P[:, :].rearrange('p (r c) -> p r c', r=4), eng=('act', 'dve')[(r // 4) % 2])
        for g in range(RC // 4):
            PS = self.pC[g % 2]
            p.mm(PS[0:32, :], self.F1i, self.rT[:, 4 * g:4 * g + 4, :])
            epilogue(g, PS[0:32, :].rearrange('p (r c) -> p r c', r=4))


def hyena_program(p, ins, outs, R):
    with p.scope():
        H = HyenaConv(p, ins['fconst'])
        sb = p.sb
        XA, XB = sb('hXA', [64, RC, 128]), sb('hXB', [64, RC, 128])
        V, X1, X2, Z, O = (sb('h' + n, [32, RC, 128]) for n in ('V', 'X1', 'X2', 'Z', 'O'))
        tt_ = sb('htt', [32, 4, 128])
        sph = [(sb('hsr%d' % o, [128, RC, 64]), sb('hsi%d' % o, [128, RC, 64])) for o in range(2)]
        ure, uim = sb('hure', [128, RC, 64]), sb('huim', [128, RC, 64])
        pre, pim = sb('hpre', [128, RC, 64]), sb('hpim', [128, RC, 64])
        skp = [sb('hskp%d' % o, [32, R]) for o in range(2)]
        for o in range(2):
            p.dma(skp[o][:], ins['skip'][o:o + 1, :].broadcast_to([32, R]), q='act')
        for ch in range(R // RC):
            rows = slice(ch * RC, (ch + 1) * RC)
            for o in range(2):
                p.dma(XA[:], ins['ha'][o, rows, :].rearrange('r (a q) -> a r q', a=64))
                p.dma(XB[:], ins['hb'][o, rows, :].rearrange('r (a q) -> a r q', a=64), q='act')
                p.tt(XA[:], XA[:], XB[:], ALU.add)
                H.fwd(XA, 64, sph[o][0], sph[o][1])
            for T_, nm in ((V, 'v'), (X1, 'x1'), (X2, 'x2')):
                p.dma(T_[:], ins[nm][rows, :].rearrange('r (a q) -> a r q', a=32), q='act')

            def mk_epi(gate, base, skt, dst):
                def epi(g, PS):
                    rs = slice(4 * g, 4 * g + 4)
                    sk = skt[:, ch * RC + 4 * g:ch * RC + 4 * g + 4].unsqueeze(2).broadcast_to([32, 4, 128])
                    p.tt(tt_[:], base[:, rs, :], sk, ALU.mult, eng='pool')
                    p.tt(tt_[:], tt_[:], PS, ALU.add)
                    p.tt(dst[:, rs, :], tt_[:], gate[:, rs, :], ALU.mult)
                return epi
            H.fwd(V, 32, ure, uim)
            H.mul(pre[:], pim[:], ure[:], uim[:], sph[0][0][:], sph[0][1][:])
            H.inv(pre, pim, mk_epi(X1, V, skp[0], Z))
            H.fwd(Z, 32, ure, uim)
            H.mul(pre[:], pim[:], ure[:], uim[:], sph[1][0][:], sph[1][1][:])
            H.inv(pre, pim, mk_epi(X2, Z, skp[1], O))
            p.dma(outs['y'][rows, :].rearrange('r (a q) -> a r q', a=32), O[:], q='pool')


def layer_even(x2, W, B, T, n):
    ntok = B * T // n
    ident = np.eye(128, dtype=np.float32)
    xh = shard_halo(x2, B, T, n, halo=2)
    pk = ['hy_conv_w', 'hy_conv_b', 'gdn_conv_w', 'gdn_a_log', 'gdn_dt_bias']
    in_maps = [dict({'xh': xh[c], 'w_in': f32c(W['w_in']), 'ident': ident}, **{k: f32c(W[k]) for k in pk}) for c in range(n)]
    onames = {'hx1': 512, 'hx2': 512, 'hv': 512, 'gv': 512, 'gq': 512, 'gkk': 512, 'gkb': 512, 'gka': 1024, 'glw': 1024, 'zg': 512}

    def body1(p, ins, outs, scr):
        idn = load_ident(p, ins)
        linear_program(p, idn, ins['xh'], ins['w_in'], scr['projh'], ntok + 128, D, 3596, tag='li')
        p.dma(outs['zg'], scr['projh'][2:ntok + 2, 3072:3584])
        prep_even(p, scr['projh'], ins, outs, ntok)
    r1 = launch(body1, in_maps, {k: (ntok, v) for k, v in onames.items()}, {'projh': (ntok + 128, 3596)})
    full = {k: np.concatenate([r1[c][k] for c in range(n)], 0).reshape(B, T, -1) for k in onames}

    z, window = filter_consts(T)
    npos = T // n
    bf = np.zeros((64, 128), np.float32)
    bf[:, 0], bf[:, 1], bf[:, 2], bf[:, 3] = W['f_b1'], W['f_b2'], W['f_b3'], W['f_freq']
    in_maps = [dict(zT=f32c(z[c * npos:(c + 1) * npos].T), window=f32c(window[c * npos:(c + 1) * npos]), f_bf=bf,
                    f_w1=f32c(W['f_w1']), f_w2=f32c(W['f_w2']), f_w3=f32c(W['f_w3']), f_w4=f32c(W['f_w4'])) for c in range(n)]
    r2 = launch(lambda p, ins, outs, scr: filter_program(p, ins, outs, npos), in_maps, {'h': (npos, 2048)})
    hf = np.concatenate([r2[c]['h'] for c in range(n)], 0).reshape(T, 2, 2, 512)

    Rr = B * 512 // n
    rowsT = {k: f32c(np.transpose(full[k], (0, 2, 1)).reshape(B * 512, T)) for k in ('hv', 'hx1', 'hx2')}
    ha = np.zeros((2, 512, 2 * T), np.float32)
    hb = np.zeros((2, 512, 2 * T), np.float32)
    for o in range(2):
        ha[o, :, 0:T] = hf[:, o, 0, :].T
        ha[o, :, T + 1:] = hf[:0:-1, o, 1, :].T
        hb[o, :, 0] = hf[0, o, 1, :]
    fc = fft_consts()
    in_maps = []
    for c in range(n):
        rs = slice(c * Rr, (c + 1) * Rr)
        ch = np.arange(c * Rr, (c + 1) * Rr) % 512
        in_maps.append(dict(v=rowsT['hv'][rs], x1=rowsT['hx1'][rs], x2=rowsT['hx2'][rs], ha=f32c(ha[:, ch]), hb=f32c(hb[:, ch]),
                            skip=f32c(W['hy_skip'][:, ch]), fconst=fc))
    r3 = launch(lambda p, ins, outs, scr: hyena_program(p, ins, outs, Rr), in_maps, {'y': (Rr, T)})
    ya = np.concatenate([r3[c]['y'] for c in range(n)], 0).reshape(B, 512, T).transpose(0, 2, 1)

    ub = []
    for b in range(B):
        for zz in range(2):
            for h in range(4):
                hs = slice(128 * h, 128 * (h + 1))
                zs = slice(512 * zz + 128 * h, 512 * zz + 128 * (h + 1))
                arrs = [full['gq'][b, :, hs], full['gkb'][b, :, hs], full['gv'][b, :, hs], full['glw'][b, :, zs],
                        full['gkk'][b, :, hs], full['gka'][b, :, zs]]
                ub.append(np.stack([a[::-1] if zz else a for a in arrs]))
    _, yb = run_recur([], None, ub, (128, 128, True), T, n)
    o = np.zeros((2, B, T, 512), np.float32)
    i = 0
    for b in range(B):
        for zz in range(2):
            for h in range(4):
                o[zz, b, :, 128 * h:128 * (h + 1)] = yb[i][::-1] if zz else yb[i]
                i += 1

    def sh(a):
        return np.split(f32c(a.reshape(B * T, -1)), n, 0)
    parts = {'x': sh(x2), 'of': sh(o[0]), 'ob': sh(o[1]), 'zg': sh(full['zg']), 'ya': sh(ya)}
    wk = ['w_out', 'ln_g', 'ln_b', 'mlp_w1', 'mlp_w2', 'gdn_norm_w']
    in_maps = [dict({k: f32c(v[c]) for k, v in parts.items()}, ident=ident, **{k: f32c(W[k]) for k in wk}) for c in range(n)]

    def body5(p, ins, outs, scr):
        idn = load_ident(p, ins)
        post_even(p, ins, ins, scr['ymix'], ntok)
        dense_tail(p, idn, ins['x'], scr['ymix'], ins, outs['xo'], scr, ntok, 0)
    r5 = launch(body5, in_maps, {'xo': (ntok, D)},
                {'ymix': (ntok, D), 'mix': (ntok, D), 'x1': (ntok, D), 'hmid': (ntok, 4 * D), 'mlp': (ntok, D)})
    return np.concatenate([r5[c]['xo'] for c in range(n)], 0)


def kernel(**inp):
    B, T, n = 4, 4096, 8
    g = lambda k: np.asarray(inp[k], dtype=np.float32)
    x2 = f32c(g('x')).reshape(B * T, D)
    W0 = dict(w_in=g('ev_w_in')[0], hy_conv_w=g('hy_conv_w')[0], hy_conv_b=g('hy_conv_b')[0], gdn_conv_w=g('gdn_conv_w')[0],
              gdn_a_log=g('gdn_a_log')[0].reshape(-1), gdn_dt_bias=g('gdn_dt_bias')[0].reshape(-1), gdn_norm_w=g('gdn_norm_w')[0],
              f_w1=g('hy_filt_w1')[0], f_b1=g('hy_filt_b1')[0], f_w2=g('hy_filt_w2')[0], f_b2=g('hy_filt_b2')[0],
              f_w3=g('hy_filt_w3')[0], f_b3=g('hy_filt_b3')[0], f_w4=g('hy_filt_w4')[0], f_freq=g('hy_filt_freq')[0],
              hy_skip=g('hy_skip')[0], w_out=g('ev_w_out')[0], ln_g=g('ln_g')[0], ln_b=g('ln_b')[0],
              mlp_w1=g('mlp_w1')[0], mlp_w2=g('mlp_w2')[0])
    W1 = dict(w_in=g('od_w_in')[0], hg_lower=g('hg_lower'), hg_norm_w=g('hg_norm_w')[0], rw_mu=g('rw_mu')[0], rw_w0=g('rw_w0')[0],
              rw_w2=g('rw_w2')[0], rw_a0=g('rw_a0')[0], rw_a2=g('rw_a2')[0], rw_g2=g('rw_g2')[0], rw_k_k=g('rw_k_k')[0],
              rw_k_a=g('rw_k_a')[0], rw_r_k=g('rw_r_k')[0].reshape(-1), rw_lnx_w=g('rw_lnx_w')[0], rw_lnx_b=g('rw_lnx_b')[0],
              w_out=g('od_w_out')[0], ln_g=g('ln_g')[1], ln_b=g('ln_b')[1], mlp_w1=g('mlp_w1')[1], mlp_w2=g('mlp_w2')[1])
    x2 = layer_even(x2, W0, B, T, n)
    x2 = layer_odd(x2, W1, B, T, n)
    return f32c(x2.reshape(B, T, D))
```
